# Optimizing a Trainium2 kernel written in Bass

```python
import jax, jax.numpy as jnp
from jax import lax
import numpy as np


D_MODEL = 1024
BATCH = 16
SEQ = 4096
DEPTH = 1
DEC_BATCH = 128
DEC_SEQ = 1
PAST_LEN = 8192
PAGE_SIZE = 128

CONV_CH = D_MODEL // 2
CONV_K = 3
N_HEADS = 8
HEAD_DIM = D_MODEL // 16
ATTN_WIDTH = N_HEADS * HEAD_DIM
MIX_WIDTH = CONV_CH + ATTN_WIDTH
IN_WIDTH = 3 * CONV_CH + 3 * ATTN_WIDTH
D_FF = 2816
ROPE_THETA = 10000.0
EPS = 1e-6
PATTERNS = ((128, 1), (512, 4), (2048, 16))
WIN_MAX = max(w for w, _ in PATTERNS)

kernel_name = 'hymba_conv_dilated_swa_macaron_step'


def rmsnorm(x, g):
    xf = x.astype(jnp.float32)
    y = xf * lax.rsqrt(jnp.mean(xf * xf, axis=-1, keepdims=True) + EPS)
    return (y * g.astype(jnp.float32)).astype(x.dtype)


def half_ffn(x, g, w1, w3, w2):
    h = rmsnorm(x, g)
    return x + 0.5 * ((jax.nn.silu(h @ w1) * (h @ w3)) @ w2)


def rope(x, pos):
    dh = x.shape[-1]
    half = dh // 2
    inv = ROPE_THETA ** (-jnp.arange(half, dtype=jnp.float32) * 2.0 / dh)
    ang = pos[:, None] * inv[None, :]
    cos = jnp.cos(ang)[None, :, None, :]
    sin = jnp.sin(ang)[None, :, None, :]
    xf = x.astype(jnp.float32)
    x1, x2 = xf[..., :half], xf[..., half:]
    return jnp.concatenate([x1 * cos - x2 * sin, x2 * cos + x1 * sin], axis=-1).astype(x.dtype)


def split_mix(z):
    W, A = CONV_CH, ATTN_WIDTH
    b_g = z[..., :W]
    c_g = z[..., W:2 * W]
    x_in = z[..., 2 * W:3 * W]
    o = 3 * W
    heads = z.shape[:-1] + (N_HEADS, HEAD_DIM)
    q = z[..., o:o + A].reshape(heads)
    k = z[..., o + A:o + 2 * A].reshape(heads)
    v = z[..., o + 2 * A:o + 3 * A].reshape(heads)
    return b_g, c_g, x_in, q, k, v


def causal_conv(u_ext, w):
    s = u_ext.shape[1] - (CONV_K - 1)
    return sum(w[j] * u_ext[:, j:j + s] for j in range(CONV_K))


def dilated_band_prompt(q, k, v, window, dilation):
    b, s, h, dh = q.shape
    nb = window // dilation
    seg = nb * dilation
    s_pad = -(-s // seg) * seg
    n_blk = s_pad // seg

    def to_blocks(t):
        t = jnp.pad(t, ((0, 0), (0, s_pad - s), (0, 0), (0, 0)))
        t = t.reshape(b, s_pad // dilation, dilation, h, dh).transpose(0, 2, 1, 3, 4)
        return t.reshape(b, dilation, n_blk, nb, h, dh)

    def with_prev(t):
        prev = jnp.pad(t[:, :, :-1], ((0, 0), (0, 0), (1, 0), (0, 0), (0, 0), (0, 0)))
        return jnp.concatenate([prev, t], axis=3)

    qb = to_blocks(q)
    kb = with_prev(to_blocks(k))
    vb = with_prev(to_blocks(v))
    scores = jnp.einsum('brnqhd,brnkhd->brnhqk', qb, kb,
                        preferred_element_type=jnp.float32) * (dh ** -0.5)
    qi = jnp.arange(nb)[:, None]
    kj = jnp.arange(2 * nb)[None, :]
    dist = qi + nb - kj
    band = (dist >= 0) & (dist <= nb)
    first = jnp.arange(n_blk)[:, None, None] == 0
    valid = band[None] & ~(first & (kj[None] < nb))
    scores = jnp.where(valid[None, None, :, None], scores, -jnp.inf)
    lse = jax.nn.logsumexp(scores, axis=-1)
    p = jnp.exp(scores - lse[..., None])
    o = jnp.einsum('brnhqk,brnkhd->brnqhd', p, vb.astype(jnp.float32))
    o = o.reshape(b, dilation, s_pad // dilation, h, dh).transpose(0, 2, 1, 3, 4)
    o = o.reshape(b, s_pad, h, dh)[:, :s]
    lse = lse.transpose(0, 1, 2, 4, 3).reshape(b, dilation, s_pad // dilation, h)
    lse = lse.transpose(0, 2, 1, 3).reshape(b, s_pad, h)[:, :s]
    return o, lse


def dilated_gather_sample(q, k_ext, v_ext, window, dilation):
    b, ds, h, dh = q.shape
    L = k_ext.shape[1]
    wbuf = L - ds
    n_off = window // dilation + 1
    idx = wbuf + jnp.arange(ds)[:, None] - jnp.arange(n_off)[None, :] * dilation
    valid = idx >= 0
    idx = jnp.clip(idx, 0, L - 1)
    kg = k_ext[:, idx]
    vg = v_ext[:, idx]
    scores = jnp.einsum('bqhd,bqjhd->bqhj', q, kg,
                        preferred_element_type=jnp.float32) * (dh ** -0.5)
    scores = jnp.where(valid[None, :, None, :], scores, -jnp.inf)
    lse = jax.nn.logsumexp(scores, axis=-1)
    p = jnp.exp(scores - lse[..., None])
    o = jnp.einsum('bqhj,bqjhd->bqhd', p, vg.astype(jnp.float32))
    return o, lse


def combine_patterns(outs, lses):
    w = jax.nn.softmax(jnp.stack(lses, axis=0), axis=0)
    return jnp.sum(w[..., None] * jnp.stack(outs, axis=0), axis=0)


def mixer_prompt(h, w_in, conv_w):
    b_g, c_g, x_in, q, k, v = split_mix(h @ w_in)
    pos = jnp.arange(h.shape[1], dtype=jnp.float32)
    q, k = rope(q, pos), rope(k, pos)
    u = c_g * x_in
    conv_out = causal_conv(jnp.pad(u, ((0, 0), (CONV_K - 1, 0), (0, 0))), conv_w)
    res = [dilated_band_prompt(q, k, v, w, d) for (w, d) in PATTERNS]
    attn = combine_patterns([r[0] for r in res], [r[1] for r in res])
    keep = min(WIN_MAX, h.shape[1])
    state = (u[:, -(CONV_K - 1):], k[:, -keep:], v[:, -keep:])
    return b_g, conv_out, attn.astype(h.dtype), state


def mixer_sample(h, conv_state, k_buf, v_buf, w_in, conv_w):
    b_g, c_g, x_in, q, k, v = split_mix(h @ w_in)
    pos = PAST_LEN + jnp.arange(h.shape[1], dtype=jnp.float32)
    q, k = rope(q, pos), rope(k, pos)
    u = c_g * x_in
    u_ext = jnp.concatenate([conv_state.astype(u.dtype), u], axis=1)
    conv_out = causal_conv(u_ext, conv_w)
    k_ext = jnp.concatenate([k_buf.astype(k.dtype), k], axis=1)
    v_ext = jnp.concatenate([v_buf.astype(v.dtype), v], axis=1)
    res = [dilated_gather_sample(q, k_ext, v_ext, w, d) for (w, d) in PATTERNS]
    attn = combine_patterns([r[0] for r in res], [r[1] for r in res])
    wbuf = k_buf.shape[1]
    state = (u_ext[:, -(CONV_K - 1):], k_ext[:, -wbuf:], v_ext[:, -wbuf:])
    return b_g, conv_out, attn.astype(h.dtype), state


def trunk_layer(x, ffn1, mixw, ffn2, past):
    g_mix, w_in, conv_w, g_conv, g_attn, w_out = mixw
    x = half_ffn(x, *ffn1)
    h = rmsnorm(x, g_mix)
    if past is None:
        b_g, conv_out, attn, state = mixer_prompt(h, w_in, conv_w)
    else:
        b_g, conv_out, attn, state = mixer_sample(h, past[0], past[1], past[2], w_in, conv_w)
    y_c = rmsnorm(b_g * conv_out, g_conv)
    y_a = rmsnorm(attn.reshape(attn.shape[:2] + (ATTN_WIDTH,)), g_attn)
    x = x + jnp.concatenate([y_c, y_a], axis=-1) @ w_out
    x = half_ffn(x, *ffn2)
    return x, state


def setup_inputs(seed: int = 0) -> dict:
    key = jax.random.key(seed)
    ks = jax.random.split(key, 24)
    f32 = jnp.float32
    wbuf = min(WIN_MAX, PAST_LEN)

    def nrm(k, shape, scale):
        return jax.random.normal(k, shape, f32) * scale

    def gain(k, shape):
        return 1.0 + 0.01 * jax.random.normal(k, shape, f32)

    return {
        'x_prompt': nrm(ks[0], (BATCH, SEQ, D_MODEL), 1.0),
        'x_sample': nrm(ks[1], (DEC_BATCH, DEC_SEQ, D_MODEL), 1.0),
        'state_conv': nrm(ks[2], (DEPTH, DEC_BATCH, CONV_K - 1, CONV_CH), 1.0),
        'cache_k': nrm(ks[3], (DEPTH, DEC_BATCH, wbuf, N_HEADS, HEAD_DIM), 1.0),
        'cache_v': nrm(ks[4], (DEPTH, DEC_BATCH, wbuf, N_HEADS, HEAD_DIM), 1.0),
        'norm_ffn1': gain(ks[5], (DEPTH, D_MODEL)),
        'ffn1_w1': nrm(ks[6], (DEPTH, D_MODEL, D_FF), D_MODEL ** -0.5),
        'ffn1_w3': nrm(ks[7], (DEPTH, D_MODEL, D_FF), D_MODEL ** -0.5),
        'ffn1_w2': nrm(ks[8], (DEPTH, D_FF, D_MODEL), D_FF ** -0.5),
        'norm_mix': gain(ks[9], (DEPTH, D_MODEL)),
        'w_in': nrm(ks[10], (DEPTH, D_MODEL, IN_WIDTH), D_MODEL ** -0.5),
        'conv_w': nrm(ks[11], (DEPTH, CONV_K, CONV_CH), CONV_K ** -0.5),
        'out_norm_conv': gain(ks[12], (DEPTH, CONV_CH)),
        'out_norm_attn': gain(ks[13], (DEPTH, ATTN_WIDTH)),
        'w_out': nrm(ks[14], (DEPTH, MIX_WIDTH, D_MODEL), MIX_WIDTH ** -0.5),
        'norm_ffn2': gain(ks[15], (DEPTH, D_MODEL)),
        'ffn2_w1': nrm(ks[16], (DEPTH, D_MODEL, D_FF), D_MODEL ** -0.5),
        'ffn2_w3': nrm(ks[17], (DEPTH, D_MODEL, D_FF), D_MODEL ** -0.5),
        'ffn2_w2': nrm(ks[18], (DEPTH, D_FF, D_MODEL), D_FF ** -0.5),
        'norm_final': gain(ks[19], (D_MODEL,)),
    }


def reference(x_prompt, x_sample, state_conv, cache_k, cache_v, norm_ffn1, ffn1_w1, ffn1_w3,
              ffn1_w2, norm_mix, w_in, conv_w, out_norm_conv, out_norm_attn, w_out, norm_ffn2,
              ffn2_w1, ffn2_w3, ffn2_w2, norm_final):
    yp, ys = x_prompt, x_sample
    st_p, st_s = [], []
    for l in range(DEPTH):
        ffn1 = (norm_ffn1[l], ffn1_w1[l], ffn1_w3[l], ffn1_w2[l])
        ffn2 = (norm_ffn2[l], ffn2_w1[l], ffn2_w3[l], ffn2_w2[l])
        mixw = (norm_mix[l], w_in[l], conv_w[l], out_norm_conv[l], out_norm_attn[l], w_out[l])
        yp, sp = trunk_layer(yp, ffn1, mixw, ffn2, None)
        ys, ss = trunk_layer(ys, ffn1, mixw, ffn2, (state_conv[l], cache_k[l], cache_v[l]))
        st_p.append(sp)
        st_s.append(ss)
    y_prompt = rmsnorm(yp, norm_final)
    y_sample = rmsnorm(ys, norm_final)
    state_conv_prompt = jnp.stack([s[0] for s in st_p], axis=0)
    state_conv_sample = jnp.stack([s[0] for s in st_s], axis=0)
    cache_k_prompt = jnp.stack([s[1] for s in st_p], axis=0)
    cache_v_prompt = jnp.stack([s[2] for s in st_p], axis=0)
    cache_k_sample = jnp.stack([s[1] for s in st_s], axis=0)
    cache_v_sample = jnp.stack([s[2] for s in st_s], axis=0)
    return (y_prompt, y_sample, state_conv_prompt, state_conv_sample, cache_k_prompt, cache_v_prompt, cache_k_sample, cache_v_sample)
```

```python
from contextlib import ExitStack
import numpy as np
import ml_dtypes
import concourse.bass as bass
import concourse.mybir as mybir
from concourse.bass_utils import run_bass_kernel_spmd

F32 = mybir.dt.float32
BF16 = mybir.dt.bfloat16
AF = mybir.ActivationFunctionType
ALU = mybir.AluOpType
AX = mybir.AxisListType

D = 1024
DFF = 2816
NFC = 22
NPAIR = 11
T = 512
EPS = 1e-6
PATTERNS = ((128, 1), (512, 4), (2048, 16))
WBUF = 2048
PAST_LEN = 8192
GW = 2944
VW = 66


class Cfg:
    def __init__(self, nseq=2, seq=4096, ns=16, do_sample=True, do_cachecopy=True, stage=99):
        self.stage = stage
        self.nseq = nseq
        self.seq = seq
        self.ns = ns
        self.nt = seq // T
        self.keep = min(WBUF, seq)
        self.do_sample = do_sample
        self.do_cachecopy = do_cachecopy


class _Op:
    __slots__ = ("eng", "fn", "deps", "sig", "tok", "dma", "idx")


class _Res:
    __slots__ = ("writer", "readers")

    def __init__(self):
        self.writer = None
        self.readers = {}


class Sched:
    ENGS = ("pe", "act", "dve", "pool", "sp")

    def __init__(self):
        self.ops = {e: [] for e in self.ENGS}
        self.res = {}
        self.dma_cnt = {}

    def add(self, eng, fn, reads=(), writes=(), dma_sem=None, ndma=1):
        op = _Op()
        op.eng = eng
        op.fn = fn
        op.sig = False
        op.dma = dma_sem is not None
        op.tok = None
        psr = [r for r in reads if isinstance(r, tuple) and r[0] == "ps"]
        if psr:
            writes = list(writes) + [r for r in psr if r not in writes]
        deps = {}
        for r in reads:
            rs = self.res.get(r)
            if rs is not None and rs.writer is not None:
                deps[id(rs.writer)] = rs.writer
        for w in writes:
            rs = self.res.get(w)
            if rs is not None:
                if rs.writer is not None:
                    deps[id(rs.writer)] = rs.writer
                for rd in rs.readers.values():
                    deps[id(rd)] = rd
        for r in reads:
            rs = self.res.setdefault(r, _Res())
            key = dma_sem if op.dma else eng
            rs.readers[key] = op
        for w in writes:
            rs = self.res.setdefault(w, _Res())
            rs.writer = op
            rs.readers = {}
        op.deps = []
        for d in deps.values():
            if d is op:
                continue
            if (not d.dma) and d.eng == eng and eng == "pe":
                continue
            d.sig = True
            op.deps.append(d)
        if op.dma:
            c = self.dma_cnt.get(dma_sem, 0) + 16 * ndma
            self.dma_cnt[dma_sem] = c
            op.tok = (dma_sem, c)
        self.ops[eng].append(op)
        return op

    def finalize(self):
        for e in self.ENGS:
            c = 0
            for op in self.ops[e]:
                if not op.dma and op.sig:
                    c += 1
                    op.tok = ("eng_" + e, c)

    def sem_keys(self):
        keys = ["eng_" + e for e in self.ENGS]
        keys += list(self.dma_cnt.keys())
        return keys

    def emit(self, eng_name, eng, sems, extra_final_waits=()):
        waited = {}
        for op in self.ops[eng_name]:
            need = {}
            for d in op.deps:
                k, v = d.tok
                if waited.get(k, 0) >= v:
                    continue
                if need.get(k, 0) < v:
                    need[k] = v
            for k, v in need.items():
                eng.wait_ge(sems[k], v)
                waited[k] = v
            last = op.fn(eng)
            if not op.dma and op.sig:
                last.then_inc(sems["eng_" + eng_name], 1)
        for k, v in extra_final_waits:
            if waited.get(k, 0) < v:
                eng.wait_ge(sems[k], v)
                waited[k] = v


def build_program(cfg):
    nc = bass.Bass("TRN2", target_bir_lowering=False)
    S = Sched()
    NSEQ, SEQ, NS, NT = cfg.nseq, cfg.seq, cfg.ns, cfg.nt
    KEEP = cfg.keep

    def din(name, shape, dt=F32):
        return nc.dram_tensor(name, list(shape), dt, kind="ExternalInput").ap()

    def dout(name, shape, dt=F32):
        return nc.dram_tensor(name, list(shape), dt, kind="ExternalOutput").ap()

    def dscr(name, shape, dt=BF16):
        return nc.dram_tensor(name, list(shape), dt, kind="Internal").ap()

    xp = din("xp", [NSEQ, SEQ, D])
    xs = din("xs", [NS, D])
    sconv = din("sconv", [NS, 2, 512])
    ck = din("ck", [NS, WBUF, 512])
    cv = din("cv", [NS, WBUF, 512])
    w1 = [din("w1a", [D, DFF]), din("w1b", [D, DFF])]
    w3 = [din("w3a", [D, DFF]), din("w3b", [D, DFF])]
    w2 = [din("w2a", [DFF, D]), din("w2b", [DFF, D])]
    win = din("win", [D, 3072])
    wout = din("wout", [D, D])
    gvec = din("gvec", [128, 32])
    cwv = din("cwv", [128, 12])
    gfin = din("gfin", [1, D])
    ident_d = din("ident", [128, 128], BF16)
    gmask_d = din("gmask", [128, GW], BF16)
    csp_d = din("csp", [SEQ, 64])
    css_d = din("css", [1, 64])
    identf_d = din("identf", [128, 128])

    yp = dout("yp", [NSEQ, SEQ, D])
    ys = dout("ys", [NS, D])
    scp = dout("scp", [NSEQ, 2, 512])
    scs = dout("scs", [NS, 2, 512])
    ckp = dout("ckp", [NSEQ, KEEP, 512])
    cvp = dout("cvp", [NSEQ, KEEP, 512])
    cks = dout("cks", [NS, WBUF, 512])
    cvs = dout("cvs", [NS, WBUF, 512])

    ffA = [dscr("ffA0", [NPAIR, 2, 128, 2048]), dscr("ffA1", [NPAIR, 2, 128, 2048])]
    ffB = [dscr("ffB0", [NPAIR, 128, 2048]), dscr("ffB1", [NPAIR, 128, 2048])]
    wcv = dscr("wcv", [6, 128, 2048])
    wv = dscr("wv", [2, 128, 2048])
    wqk = dscr("wqk", [4, 128, 2048])
    wo = dscr("wo", [4, 128, 2048])

    es = ExitStack()

    def sb(name, shape, dt):
        return es.enter_context(nc.sbuf_tensor(name, list(shape), dt))

    wring = sb("wring", [128, 6, 2048], BF16)
    xres = sb("xres", [128, 6, 1024], F32)
    hb = sb("hb", [128, 2, 1024], BF16)
    hT = sb("hT", [128, 8, 512], BF16)
    gT = sb("gT", [128, NFC, 512], BF16)
    sa = sb("sa", [128, 2, 512], F32)
    KT = sb("KT", [128, 4, 2560], BF16)
    VX = sb("VX", [128, 20, 8, VW], BF16)
    QT = sb("QT", [128, 8, 512], BF16)
    G = sb("G", [128, GW], BF16)
    gfb = sb("gfb", [128, D], F32)
    cs = sb("cs", [128, 2, 4, 64], F32)
    ubuf = sb("ubuf", [128, 4, 514], F32)
    csb = sb("csb", [128, 2, 512], F32)
    cacc = sb("cacc", [128, 1, 512], F32)
    ycb = sb("ycb", [128, 1, 512], F32)
    sqb = sb("sqb", [128, 1, 512], BF16)
    ycgT = sb("ycgT", [128, 4, 512], BF16)
    rtA = sb("rtA", [128, 1, 512], F32)
    rtB = sb("rtB", [128, 1, 512], F32)
    kout = sb("kout", [128, 4, 512], F32)
    vout = sb("vout", [128, 2, 256], F32)
    qb = sb("qb", [128, 2, 2, 128], BF16)
    kb = sb("kb", [128, 2, 2, 128], BF16)
    pT = sb("pT", [128, 6, 512], BF16)
    attn = sb("attn", [128, 4, 512], F32)
    ya = sb("ya", [128, 2, 512], BF16)
    yaT = sb("yaT", [128, 4, 512], BF16)
    ident = sb("identb", [128, 128], BF16)
    gcol = sb("gcol", [128, 32], F32)
    cwc = sb("cwc", [128, 12], F32)
    ones = sb("ones", [128, 1], BF16)
    epsc = sb("epsc", [128, 1], F32)
    junk = sb("junk", [128, 1024], BF16)
    st = sb("st", [128, 128], F32)
    rec = sb("rec", [128, 2, 4], F32)
    css = sb("css_s", [128, 64], F32)
    gflat0 = gT[:, :, :].rearrange("p a b -> p (a b)")
    qtok = gflat0[0:16, 8192:9216].bitcast(F32)
    vnew = gflat0[0:16, 9216:10240].bitcast(F32)
    GRES = [("gT", fc_) for fc_ in range(NFC)]
    zsel = sb("zsel", [128, 32], BF16)

    ps = [es.enter_context(nc.psum_tensor("ps%d" % i, [128, 512], F32)) for i in range(8)]

    big_rr = [0]

    def big_bank():
        b = big_rr[0] % 4
        big_rr[0] += 1
        return b

    nconst = [0]

    def const_load(out_ap, in_ap):
        def fn(e):
            return e.dma_start(out=out_ap, in_=in_ap).then_inc(SEMS["const"], 16)
        S.add("sp", fn, dma_sem="const")
        nconst[0] += 1

    SEMS = {}

    const_load(ident[:], ident_d[:, :])
    const_load(G[:], gmask_d[:, :])
    const_load(gcol[:], gvec[:, :])
    const_load(cwc[:], cwv[:, :])
    const_load(gfb[:].unsqueeze(1), gfin[0:1, :].partition_broadcast(128))
    const_load(css[:].unsqueeze(1), css_d[0:1, :].partition_broadcast(128))

    def init_fn(e):
        e.memset(ones[:], 1.0)
        e.memset(epsc[:], EPS)
        e.memset(ubuf[:], 0.0)
        e.memset(QT[:], 0.0)
        e.memset(zsel[:, 0:15], 0.0)
        e.memset(zsel[:, 16:32], 0.0)
        e.memset(zsel[:, 15:16], 1.0)
        return e.memset(VX[:], 1.0)

    xslot_ctr = [0]

    def xload(seq_i, ti, blocks, slots):
        for bi in blocks:
            slot = xslot_ctr[0] % 6
            xslot_ctr[0] += 1
            t0 = ti * T + bi * 128

            def fn(e, slot=slot, t0=t0):
                return e.dma_start(out=xres[:, slot, :], in_=xp[seq_i, t0:t0 + 128, :]).then_inc(SEMS["x%d" % slot], 16)
            S.add("pool", fn, writes=[("x", slot)], dma_sem="x%d" % slot)
            slots.append(slot)

    tiles = [(s_, ti_) for s_ in range(NSEQ) for ti_ in range(NT)]
    pending_slots = []
    xload(0, 0, (0, 1, 2, 3), pending_slots)

    conv_groups = {}

    def conv_dma(group, out_ap, in_ap):
        def fn(e):
            return e.dma_start(out=out_ap, in_=in_ap, max_dma_last_dim=8192).then_inc(SEMS[group], 16)
        S.add("pool", fn, writes=[group], dma_sem=group)

    def conv_ffA(l, pr, grp):
        for w_i, wsrc in enumerate((w1[l], w3[l])):
            conv_dma(grp,
                     ffA[l][pr, w_i].rearrange("p (dc c) -> p dc c", dc=8),
                     wsrc[:, pr * 256:(pr + 1) * 256].rearrange("(dc p) c -> p dc c", p=128))

    def conv_ffB(l, pr, grp):
        conv_dma(grp,
                 ffB[l][pr].rearrange("p (f d) -> p f d", f=2),
                 w2[l][pr * 256:(pr + 1) * 256, :].rearrange("(f p) d -> p f d", p=128))

    def grpA(l, pr):
        return "cvA%d_%d" % (l, pr // 4)

    for pr in range(NPAIR):
        conv_ffA(0, pr, grpA(0, pr))
    for pr in range(NPAIR):
        conv_ffB(0, pr, "cvB0")
    cc_cols = []
    for j in range(4):
        cc_cols += [512 + j * 128, 1024 + j * 128, j * 128]
    for i in range(6):
        for half in range(2):
            col = cc_cols[2 * i + half]
            conv_dma("cvMc",
                     wcv[i, :, half * 1024:(half + 1) * 1024].rearrange("p (dc c) -> p dc c", dc=8),
                     win[:, col:col + 128].rearrange("(dc p) c -> p dc c", p=128))
    for v in range(2):
        conv_dma("cvMv", wv[v].rearrange("p (dc c) -> p dc c", dc=8),
                 win[:, 2560 + v * 256:2560 + (v + 1) * 256].rearrange("(dc p) c -> p dc c", p=128))
    for g in range(4):
        for part, base in enumerate((1536, 2048)):
            conv_dma("cvMq",
                     wqk[g].rearrange("p (dc c) -> p dc c", dc=8)[:, :, part * 128:(part + 1) * 128],
                     win[:, base + g * 128:base + (g + 1) * 128].rearrange("(dc p) c -> p dc c", p=128))
    for dq in range(4):
        conv_dma("cvMo", wo[dq].rearrange("p (c8 c) -> p c8 c", c8=8),
                 wout[:, dq * 256:(dq + 1) * 256].rearrange("(c8 p) c -> p c8 c", p=128))
    for pr in range(NPAIR):
        conv_ffA(1, pr, "cvA1_%d" % (pr // 4))
    for pr in range(NPAIR):
        conv_ffB(1, pr, "cvB1")

    wcount = [0]

    def wload(src_ap, grp):
        slot = wcount[0] % 6
        wcount[0] += 1
        rname = ("wr", slot)

        def fn(e):
            return e.dma_start(out=wring[:, slot, :], in_=src_ap).then_inc(SEMS["wr%d" % slot], 16)
        S.add("sp", fn, reads=[grp], writes=[rname], dma_sem="wr%d" % slot)
        return slot

    stat_rr = [0]

    def stat_cols(n):
        c = (stat_rr[0] % 16) * 8
        stat_rr[0] += 1
        return c

    def rms_rows(src_ap, P, width, dst_bf_ap, src_res, dst_res, extra_scale_out=None):
        c = stat_cols(3)
        m_ap = st[:P, c:c + 1]
        l_ap = st[:P, c + 1:c + 2]
        r_ap = st[:P, c + 2:c + 3]
        sres = ("st", c)

        def f1(e):
            return e.activation(out=junk[:P, 0:width], in_=src_ap, func=AF.Square,
                                scale=float(width) ** -0.5, accum_out=m_ap)
        S.add("act", f1, reads=list(src_res), writes=[sres])

        def f2(e):
            return e.activation(out=l_ap, in_=m_ap, func=AF.Ln, bias=epsc[:P, 0:1], scale=1.0)
        S.add("act", f2, reads=[sres], writes=[sres])

        def f3(e):
            return e.activation(out=r_ap, in_=l_ap, func=AF.Exp, scale=-0.5)
        S.add("act", f3, reads=[sres], writes=[sres])

        def f4(e):
            return e.activation(out=dst_bf_ap, in_=src_ap, func=AF.Copy, scale=r_ap)
        S.add("act", f4, reads=[sres] + list(src_res), writes=list(dst_res))
        return r_ap, sres

    def normA(bi, slot, P):
        hs = bi % 2
        rms_rows(xres[:P, slot, :], P, D, hb[:P, hs, :], [("x", slot)], [("hb", hs)])

    def normB(bi, P, gbase):
        hs = bi % 2

        def ft(e, hs=hs):
            for dc in range(8):
                ins = e.transpose(out=ps[6][:, :].bitcast(BF16).rearrange("p (c t) -> p c t", c=8)[:, dc, 0:P],
                                  in_=hb[:P, hs, dc * 128:(dc + 1) * 128], identity=ident[:P, :P])
            return ins
        S.add("pe", ft, reads=[("hb", hs)], writes=[("ps", 6)])

        def fc(e, bi=bi):
            src = ps[6][:, :].bitcast(BF16).rearrange("p (c t) -> p c t", c=8)[:, :, 0:P]
            gsl = gcol[:, gbase:gbase + 8].unsqueeze(2).to_broadcast([128, 8, P])
            return e.tensor_tensor(out=hT[:, :, bi * 128:bi * 128 + P], in0=src, in1=gsl, op=ALU.mult)
        S.add("dve", fc, reads=[("ps", 6)], writes=[("hT", bi)])

    def norm_to_hT(slots, P, gbase):
        nb = len(slots)
        for bi, slot in enumerate(slots):
            normA(bi, slot, 128 if bi < nb - 1 else P)
            normB(bi, 128 if bi < nb - 1 else P, gbase)

    def ffn(l, slots, P):
        nb = len(slots)
        TW = (nb - 1) * 128 + P
        hres = [("hT", bi) for bi in range(nb)]
        for pr in range(NPAIR):
            s1 = wload(ffA[l][pr, 0], grpA(l, pr))
            s3 = wload(ffA[l][pr, 1], grpA(l, pr))
            for f2 in range(2):
                fc = pr * 2 + f2
                ba = fc % 2
                bb = 2 + fc % 2

                def fm(e, s1=s1, s3=s3, f2=f2, ba=ba, bb=bb):
                    for dc in range(8):
                        e.matmul(ps[ba][:, 0:TW], lhsT=wring[:, s1, dc * 256 + f2 * 128: dc * 256 + f2 * 128 + 128],
                                 rhs=hT[:, dc, 0:TW], start=(dc == 0), stop=(dc == 7))
                    for dc in range(8):
                        ins = e.matmul(ps[bb][:, 0:TW], lhsT=wring[:, s3, dc * 256 + f2 * 128: dc * 256 + f2 * 128 + 128],
                                       rhs=hT[:, dc, 0:TW], start=(dc == 0), stop=(dc == 7))
                    return ins
                S.add("pe", fm, reads=[("wr", s1), ("wr", s3)] + hres, writes=[("ps", ba), ("ps", bb)])
                ss = fc % 2

                def fs(e, ba=ba, ss=ss):
                    import os as _os
                    return e.activation(out=sa[:, ss, 0:TW], in_=ps[ba][:, 0:TW], func=(AF.Copy if _os.environ.get("NOSILU") else AF.Silu))
                S.add("act", fs, reads=[("ps", ba)], writes=[("sa", ss)])

                def fg(e, bb=bb, ss=ss, fc=fc):
                    return e.tensor_tensor(out=gT[:, fc, 0:TW], in0=sa[:, ss, 0:TW], in1=ps[bb][:, 0:TW], op=ALU.mult)
                S.add("dve", fg, reads=[("sa", ss), ("ps", bb)], writes=[("gT", fc)])
        import os as _os
        if _os.environ.get("FFN_P1ONLY"):
            return
        accb = lambda bi, half: (int(_os.environ.get("P2_OFF", "0")) + bi * 2 + half) % 8
        for pr in range(NPAIR):
            s2 = wload(ffB[l][pr], "cvB%d" % l)

            def fm2(e, s2=s2, pr=pr):
                ins = None
                for f2 in range(2):
                    fc = pr * 2 + f2
                    for bi in range(min(nb, int(_os.environ.get("P2_NBI", "9")))):
                        pw = 128 if bi < nb - 1 else P
                        for half in range(int(_os.environ.get("P2_NHALF", "2"))):
                            ins = e.matmul(ps[accb(bi, half)][0:pw, :],
                                           lhsT=gT[:, fc, bi * 128:bi * 128 + pw],
                                           rhs=wring[:, s2, f2 * 1024 + half * 512: f2 * 1024 + half * 512 + 512],
                                           start=(fc == 0), stop=(fc == NFC - 1))
                return ins
            S.add("pe", fm2, reads=[("wr", s2), ("gT", 2 * pr), ("gT", 2 * pr + 1)],
                  writes=[("ps", accb(bi, h)) for bi in range(nb) for h in range(2)])
        for bi, slot in enumerate(slots):
            if _os.environ.get("P2_NORES"):
                break
            pw = 128 if bi < nb - 1 else P
            for half in range(2):
                def fr(e, bi=bi, slot=slot, half=half, pw=pw):
                    xsl = xres[0:pw, slot, half * 512:(half + 1) * 512]
                    return e.scalar_tensor_tensor(out=xsl, in0=ps[accb(bi, half)][0:pw, :], scalar=0.5, in1=xsl,
                                                  op0=ALU.mult, op1=ALU.add)
                S.add("dve", fr, reads=[("ps", accb(bi, half)), ("x", slot)], writes=[("x", slot)])

    def conv_path(slots, P, first_tile, sample):
        nb = len(slots)
        TW = (nb - 1) * 128 + P
        hres = [("hT", bi) for bi in range(nb)]
        banks = {}
        pend_fst = []
        cacc_v = lambda s_: cacc[:, 0, 0:TW] if s_ == 0 else sa[:, 0, 0:TW]
        ycb_v = lambda s_: ycb[:, 0, 0:TW] if s_ == 0 else sa[:, 1, 0:TW]
        cacc_r = lambda s_: ("cacc", 0) if s_ == 0 else ("sa", 0)
        ycb_r = lambda s_: ("ycb", 0) if s_ == 0 else ("sa", 1)
        for i in range(6):
            sl = wload(wcv[i], "cvMc")
            for half in range(2):
                cc = 2 * i + half
                j, kind = cc // 3, cc % 3
                bk = big_bank()
                banks[kind] = bk

                def fm(e, sl=sl, half=half, bk=bk):
                    for dc in range(8):
                        ins = e.matmul(ps[bk][:, 0:TW], lhsT=wring[:, sl, half * 1024 + dc * 128: half * 1024 + dc * 128 + 128],
                                       rhs=hT[:, dc, 0:TW], start=(dc == 0), stop=(dc == 7))
                    return ins
                S.add("pe", fm, reads=[("wr", sl)] + hres, writes=[("ps", bk)])
                if kind == 2 and pend_fst:
                    pf, pr_ = pend_fst.pop(0)
                    S.add("pe", pf, reads=pr_, writes=[("ps", 7)])
                s2 = j % 2
                s1_ = j % 2
                if kind == 0:
                    def fcopy(e, bk=bk, s2=s2):
                        return e.activation(out=csb[:, s2, 0:TW], in_=ps[bk][:, 0:TW], func=AF.Copy)
                    S.add("act", fcopy, reads=[("ps", bk)], writes=[("csb", s2)])
                elif kind == 1:
                    if not sample:
                        if first_tile:
                            def fh(e, j=j):
                                return e.memset(ubuf[:, j, 0:2], 0.0)
                        else:
                            def fh(e, j=j):
                                return e.tensor_copy(out=ubuf[:, j, 0:2], in_=ubuf[:, j, 512:514])
                        S.add("dve", fh, reads=[("u", j)], writes=[("u", j)])

                    def fu(e, bk=bk, s2=s2, j=j):
                        return e.tensor_tensor(out=ubuf[:, j, 2:2 + TW], in0=csb[:, s2, 0:TW], in1=ps[bk][:, 0:TW], op=ALU.mult)
                    S.add("dve", fu, reads=[("ps", bk), ("csb", s2), ("u", j)], writes=[("u", j)])
                    if not sample:
                        def fa0(e, s2=s2, j=j, s1_=s1_):
                            return e.activation(out=cacc_v(s1_), in_=ubuf[:, j, 0:TW], func=AF.Copy,
                                                scale=cwc[:, j * 3:j * 3 + 1])
                        S.add("act", fa0, reads=[("u", j)], writes=[cacc_r(s1_)])

                        def fa1(e, s2=s2, j=j, s1_=s1_):
                            return e.scalar_tensor_tensor(out=cacc_v(s1_), in0=ubuf[:, j, 1:1 + TW],
                                                          scalar=cwc[:, j * 3 + 1:j * 3 + 2], in1=cacc_v(s1_),
                                                          op0=ALU.mult, op1=ALU.add)
                        S.add("dve", fa1, reads=[("u", j), cacc_r(s1_)], writes=[cacc_r(s1_)])

                        def fa2(e, s2=s2, j=j, s1_=s1_):
                            return e.scalar_tensor_tensor(out=cacc_v(s1_), in0=ubuf[:, j, 2:2 + TW],
                                                          scalar=cwc[:, j * 3 + 2:j * 3 + 3], in1=cacc_v(s1_),
                                                          op0=ALU.mult, op1=ALU.add)
                        S.add("dve", fa2, reads=[("u", j), cacc_r(s1_)], writes=[cacc_r(s1_)])
                    else:
                        def fa0(e, s2=s2, j=j, s1_=s1_):
                            return e.activation(out=cacc_v(s1_), in_=ubuf[:, j, 128:128 + TW], func=AF.Copy,
                                                scale=cwc[:, j * 3:j * 3 + 1])
                        S.add("act", fa0, reads=[("u", j), ("sst", j)], writes=[cacc_r(s1_)])

                        def fa1(e, s2=s2, j=j, s1_=s1_):
                            return e.scalar_tensor_tensor(out=cacc_v(s1_), in0=ubuf[:, j, 256:256 + TW],
                                                          scalar=cwc[:, j * 3 + 1:j * 3 + 2], in1=cacc_v(s1_),
                                                          op0=ALU.mult, op1=ALU.add)
                        S.add("dve", fa1, reads=[("u", j), ("sst", j), cacc_r(s1_)], writes=[cacc_r(s1_)])

                        def fa2(e, s2=s2, j=j, s1_=s1_):
                            return e.scalar_tensor_tensor(out=cacc_v(s1_), in0=ubuf[:, j, 2:2 + TW],
                                                          scalar=cwc[:, j * 3 + 2:j * 3 + 3], in1=cacc_v(s1_),
                                                          op0=ALU.mult, op1=ALU.add)
                        S.add("dve", fa2, reads=[("u", j), cacc_r(s1_)], writes=[cacc_r(s1_)])
                else:
                    def fy(e, bk=bk, s2=s2, s1_=s1_):
                        return e.tensor_tensor(out=ycb_v(s1_), in0=cacc_v(s1_), in1=ps[bk][:, 0:TW], op=ALU.mult)
                    S.add("dve", fy, reads=[("ps", bk), cacc_r(s1_)], writes=[ycb_r(s1_)])

                    def fyg(e, s2=s2, j=j, s1_=s1_):
                        return e.activation(out=ycgT[:, j, 0:TW], in_=ycb_v(s1_), func=AF.Copy,
                                            scale=gcol[:, 28 + j:29 + j])
                    S.add("act", fyg, reads=[ycb_r(s1_)], writes=[("ycgT", j)])

                    def fsq(e, s2=s2, s1_=s1_):
                        return e.activation(out=sqb[:, 0, 0:TW], in_=ycb_v(s1_), func=AF.Square)
                    S.add("act", fsq, reads=[ycb_r(s1_)], writes=[("sqb", 0)])

                    def fst(e, s2=s2, j=j, s1_=s1_):
                        ins = None
                        for bi in range(nb):
                            pw = 128 if bi < nb - 1 else P
                            ins = e.matmul(ps[7][0:pw, bi:bi + 1], lhsT=sqb[:, 0, bi * 128:bi * 128 + pw],
                                           rhs=ones[:, 0:1], start=(j == 0 and bi == 0), stop=(j == 3 and bi == nb - 1))
                        return ins
                    pend_fst.append((fst, [("sqb", 0)]))
        for pf, pr_ in pend_fst:
            S.add("pe", pf, reads=pr_, writes=[("ps", 7)])

    def conv_rstd(nb, P):
        c = stat_cols(8)
        PP = 128 if nb > 1 else P

        def f2(e):
            return e.activation(out=st[:PP, c:c + nb], in_=ps[7][:PP, 0:nb], func=AF.Ln, bias=epsc[:PP, 0:1], scale=1.0 / 512)
        S.add("act", f2, reads=[("ps", 7)], writes=[("st", c)])

        def f3(e):
            return e.activation(out=st[:PP, c + 4:c + 4 + nb], in_=st[:PP, c:c + nb], func=AF.Exp, scale=-0.5)
        S.add("act", f3, reads=[("st", c)], writes=[("st", c)])
        return c + 4, ("st", c)

    def rope_pair(src0, src1, cs_ap, P, nbk, outs):
        raise NotImplementedError

    def vproj(slots, P, vb_of, sample, seq_i, tile_tok0, want_out):
        nb = len(slots)
        for v in range(2):
            sl = wload(wv[v], "cvMv")
            bks = [big_bank(), big_bank()]
            for bp in range((nb + 1) // 2):
                bk = bks[bp]
                bis = [bi for bi in (2 * bp, 2 * bp + 1) if bi < nb]

                def fm(e, sl=sl, bk=bk, bis=bis):
                    ins = None
                    for bi in bis:
                        pw = 128 if bi < nb - 1 else P
                        for dc in range(8):
                            ins = e.matmul(ps[bk][0:pw, (bi % 2) * 256:(bi % 2) * 256 + 256],
                                           lhsT=hT[:, dc, bi * 128:bi * 128 + pw],
                                           rhs=wring[:, sl, dc * 256:(dc + 1) * 256], start=(dc == 0), stop=(dc == 7))
                    return ins
                S.add("pe", fm, reads=[("wr", sl)] + [("hT", bi) for bi in bis], writes=[("ps", bk)])
                import os as _os
                vmode = int(_os.environ.get("VMODE", "3"))
                for bi in bis:
                    if vmode < 1:
                        break
                    pw = 128 if bi < nb - 1 else P
                    src = ps[bk][0:pw, (bi % 2) * 256:(bi % 2) * 256 + 256]
                    if not sample:
                        vb = vb_of(bi)

                        def fv(e, src=src, vb=vb, v=v, pw=pw):
                            return e.activation(out=VX[0:pw, vb, 4 * v:4 * v + 4, 0:64],
                                                in_=src.rearrange("p (h d) -> p h d", h=4), func=AF.Copy)
                        S.add("act", fv, reads=[("ps", bk)], writes=[("VX", vb, v)])
                    if sample:
                        def fo(e, src=src, v=v, pw=pw):
                            return e.tensor_copy(out=vnew[0:pw, v * 256:(v + 1) * 256], in_=src)
                        S.add("dve", fo, reads=[("ps", bk)], writes=[("vnew", v)] + GRES)

                        def fd(e, v=v, pw=pw):
                            return e.dma_start(out=cvs[:, WBUF - 1, v * 256:(v + 1) * 256],
                                               in_=vnew[0:pw, v * 256:(v + 1) * 256]).then_inc(SEMS["snew"], 16)
                        S.add("pool", fd, reads=[("vnew", v)] + GRES, dma_sem="snew")
                    elif want_out and vmode >= 2:
                        vs = (v * 4 + bi) % 2

                        def fo(e, src=src, vs=vs, pw=pw):
                            return e.tensor_copy(out=vout[0:pw, vs, :], in_=src)
                        S.add("dve", fo, reads=[("ps", bk)], writes=[("vout", vs)])
                        if False:
                            pass
                        else:
                            r0 = tile_tok0 + bi * 128 - (SEQ - KEEP)
                            dst = cvp[seq_i, r0:r0 + 128, v * 256:(v + 1) * 256]

                        def fd(e, dst=dst, vs=vs, pw=pw):
                            return e.dma_start(out=dst, in_=vout[0:pw, vs, :]).then_inc(SEMS["vout%d" % vs], 16)
                        if vmode >= 3:
                            S.add(_os.environ.get("VQ", "pool"), fd, reads=[("vout", vs)], dma_sem="vout%d" % vs)

    qkslot = [0]

    def qkproj(slots, P, g, cs_slot, kt_col0, sample):
        nb = len(slots)
        sl = wload(wqk[g], "cvMq")
        bks = [big_bank(), big_bank()]
        for bp in range((nb + 1) // 2):
            bk = bks[bp]
            bis = [bi for bi in (2 * bp, 2 * bp + 1) if bi < nb]
            nbk = len(bis)
            PP = 128 if nb > 1 else P

            def fm(e, sl=sl, bk=bk, bis=bis):
                ins = None
                for bi in bis:
                    pw = 128 if bi < nb - 1 else P
                    for dc in range(8):
                        ins = e.matmul(ps[bk][0:pw, (bi % 2) * 256:(bi % 2) * 256 + 256],
                                       lhsT=hT[:, dc, bi * 128:bi * 128 + pw],
                                       rhs=wring[:, sl, dc * 256:(dc + 1) * 256], start=(dc == 0), stop=(dc == 7))
                return ins
            S.add("pe", fm, reads=[("wr", sl)] + [("hT", bi) for bi in bis], writes=[("ps", bk)])
            rs = 0
            qs = qkslot[0] % 2
            qkslot[0] += 1
            src = ps[bk][0:PP, 0:nbk * 256].rearrange("p (b n t f) -> p b n t f", b=nbk, n=4, t=2)
            srcf = ps[bk][0:PP, 0:nbk * 256].rearrange("p (b m f) -> p b m f", b=nbk, m=8)
            if sample:
                cos_b = css[0:PP, 0:32].unsqueeze(1).unsqueeze(1).to_broadcast([PP, nbk, 8, 32])
                sin_b = css[0:PP, 32:64].unsqueeze(1).unsqueeze(1).to_broadcast([PP, nbk, 4, 32])
            else:
                b0 = bis[0]
                cos_b = cs[0:PP, cs_slot, b0:b0 + nbk, 0:32].unsqueeze(2).to_broadcast([PP, nbk, 8, 32])
                sin_b = cs[0:PP, cs_slot, b0:b0 + nbk, 32:64].unsqueeze(2).to_broadcast([PP, nbk, 4, 32])
            tA = rtA[0:PP, rs, 0:nbk * 256].rearrange("p (b n t f) -> p b n t f", b=nbk, n=4, t=2)
            tAf = rtA[0:PP, rs, 0:nbk * 256].rearrange("p (b m f) -> p b m f", b=nbk, m=8)
            tB = rtB[0:PP, rs, 0:nbk * 256].rearrange("p (b n t f) -> p b n t f", b=nbk, n=4, t=2)

            def fr(e, src=src, srcf=srcf, cos_b=cos_b, sin_b=sin_b, tAf=tAf, tB=tB):
                e.tensor_tensor(out=tAf, in0=srcf, in1=cos_b, op=ALU.mult)
                e.tensor_tensor(out=tB[:, :, :, 0, :], in0=src[:, :, :, 1, :], in1=sin_b, op=ALU.mult)
                return e.tensor_tensor(out=tB[:, :, :, 1, :], in0=src[:, :, :, 0, :], in1=sin_b, op=ALU.mult)
            S.add("dve", fr, reads=[("ps", bk), ("cs", cs_slot)], writes=[("rt", rs)])
            if sample:
                qdst = qtok[0:PP, g * 128:(g + 1) * 128].unsqueeze(1).rearrange("p b (n t f) -> p b n t f", n=2, t=2)
            else:
                qdst = qb[0:PP, qs, 0:nbk, :].rearrange("p b (n t f) -> p b n t f", n=2, t=2)
            kdst = kout[0:PP, bis[0]:bis[0] + nbk, g * 128:(g + 1) * 128].rearrange("p b (n t f) -> p b n t f", n=2, t=2)

            def fo(e, tA=tA, tB=tB, qdst=qdst, kdst=kdst):
                e.tensor_tensor(out=qdst[:, :, :, 0, :], in0=tA[:, :, 0:2, 0, :], in1=tB[:, :, 0:2, 0, :], op=ALU.subtract)
                e.tensor_tensor(out=qdst[:, :, :, 1, :], in0=tA[:, :, 0:2, 1, :], in1=tB[:, :, 0:2, 1, :], op=ALU.add)
                e.tensor_tensor(out=kdst[:, :, :, 0, :], in0=tA[:, :, 2:4, 0, :], in1=tB[:, :, 2:4, 0, :], op=ALU.subtract)
                return e.tensor_tensor(out=kdst[:, :, :, 1, :], in0=tA[:, :, 2:4, 1, :], in1=tB[:, :, 2:4, 1, :], op=ALU.add)
            S.add("dve", fo, reads=[("rt", rs)], writes=[("qb", qs), ("qtok", g)] + [("kout", bi) for bi in bis] + (GRES if sample else []))
            if sample:
                continue

            def fkb(e, qs=qs, bis=bis, nbk=nbk):
                return e.activation(out=kb[0:PP, qs, 0:nbk, :], in_=kout[0:PP, bis[0]:bis[0] + nbk, g * 128:(g + 1) * 128],
                                    func=AF.Copy)
            S.add("act", fkb, reads=[("kout", bi) for bi in bis], writes=[("kb", qs)])
            pview = ps[6][:, :].bitcast(BF16)

            def ftr(e, qs=qs, bis=bis, nbk=nbk):
                ins = None
                for i2 in range(nbk):
                    pw = 128 if bis[i2] < nb - 1 else P
                    e.transpose(out=pview[:, i2 * 128:i2 * 128 + pw], in_=qb[0:pw, qs, i2, :], identity=ident[:pw, :pw])
                    ins = e.transpose(out=pview[:, 256 + i2 * 128:256 + i2 * 128 + pw], in_=kb[0:pw, qs, i2, :],
                                      identity=ident[:pw, :pw])
                return ins
            S.add("pe", ftr, reads=[("qb", qs), ("kb", qs)], writes=[("ps", 6)])
            wq = (nbk - 1) * 128 + (128 if bis[-1] < nb - 1 else P)
            c0 = bis[0] * 128

            def fcq(e, c0=c0, wq=wq):
                e.activation(out=QT[0:64, 2 * g, c0:c0 + wq], in_=pview[0:64, 0:wq], func=AF.Copy)
                return e.activation(out=QT[64:128, 2 * g + 1, c0:c0 + wq], in_=pview[64:128, 0:wq], func=AF.Copy)
            S.add("act", fcq, reads=[("ps", 6)], writes=[("QT", g)])
            if not sample:
                def fck(e, c0=c0, wq=wq):
                    return e.tensor_copy(out=KT[:, g, kt_col0 + c0:kt_col0 + c0 + wq], in_=pview[:, 256:256 + wq])
                S.add("dve", fck, reads=[("ps", 6)], writes=[("KT", g, kt_col0 // 512)])

    score_rr = [0]

    def attention_prompt(ti, g):
        tiles = list(range(max(0, ti - 4), ti + 1))
        kblocks = [(tj, kbi) for tj in tiles for kbi in range(4)]
        NK = len(kblocks)
        seq_ = [(hh, n_i) for hh in range(2) for n_i in range(NK)]
        LA = 4
        pend = []
        sbanks = (0, 1, 2, 3, 7)

        def normalise(h, ab):
            rsl = h % 2

            def fn1(e, ab=ab, rsl=rsl):
                acc = ps[ab][:, 0:260].rearrange("p (q c) -> p q c", q=4)
                return e.reciprocal(out=rec[:, rsl, :], in_=acc[:, :, 64])
            S.add("dve", fn1, reads=[("ps", ab)], writes=[("rec", rsl)])

            def fn2(e, ab=ab, rsl=rsl, h=h):
                acc = ps[ab][:, 0:260].rearrange("p (q c) -> p q c", q=4)
                return e.tensor_tensor(out=attn[:, :, h * 64:(h + 1) * 64], in0=acc[:, :, 0:64],
                                       in1=rec[:, rsl, :].unsqueeze(2).to_broadcast([128, 4, 64]), op=ALU.mult)
            S.add("dve", fn2, reads=[("ps", ab), ("rec", rsl)], writes=[("attn", 0), ("attn", 1), ("attn", 2), ("attn", 3)])

        for idx in range(len(seq_) + LA):
            if idx < len(seq_):
                hh, n_i = seq_[idx]
                h = 2 * g + hh
                ab = 4 + h % 2
                tj, kbi = kblocks[n_i]
                dl = ti - tj
                if dl == 0:
                    qbs = [q for q in range(4) if q >= kbi]
                elif dl == 4:
                    qbs = [q for q in range(4) if q <= kbi]
                else:
                    qbs = [0, 1, 2, 3]
                q0, q1 = qbs[0], qbs[-1] + 1
                slot = tj % 5
                bk = sbanks[score_rr[0] % 5]
                score_rr[0] += 1
                kc0 = slot * 512 + kbi * 128

                def fs(e, bk=bk, kc0=kc0, q0=q0, q1=q1, h=h):
                    return e.matmul(ps[bk][:, q0 * 128:q1 * 128], lhsT=KT[:, g, kc0:kc0 + 128],
                                    rhs=QT[:, h, q0 * 128:q1 * 128], start=True, stop=True)
                S.add("pe", fs, reads=[("KT", g, slot), ("QT", g)], writes=[("ps", bk)])
                pslot = idx % 6

                def fe(e, bk=bk, pslot=pslot, q0=q0, q1=q1):
                    return e.activation(out=pT[:, pslot, q0 * 128:q1 * 128], in_=ps[bk][:, q0 * 128:q1 * 128],
                                        func=AF.Exp, scale=0.125)
                S.add("act", fe, reads=[("ps", bk)], writes=[("pT", pslot)])
                gs = 512 * dl - 128 * kbi + 384

                def fmk(e, pslot=pslot, q0=q0, q1=q1, gs=gs):
                    return e.tensor_tensor(out=pT[:, pslot, q0 * 128:q1 * 128], in0=pT[:, pslot, q0 * 128:q1 * 128],
                                           in1=G[:, gs + q0 * 128:gs + q1 * 128], op=ALU.mult)
                S.add("dve", fmk, reads=[("pT", pslot)], writes=[("pT", pslot)])
                vb = slot * 4 + kbi
                flags = []
                for q in qbs:
                    flags.append((q, (n_i == 0 and q == qbs[0]), (n_i == NK - 1 and q == qbs[-1])))
                pend.append((h, ab, pslot, vb, flags, n_i == NK - 1))
            if idx >= LA:
                h_, ab_, pslot_, vb_, flags_, last_ = pend[idx - LA]

                def fpv(e, pslot=pslot_, vb=vb_, flags=flags_, ab=ab_, h=h_):
                    ins = None
                    acc = ps[ab][:, 0:260].rearrange("p (q c) -> p q c", q=4)
                    for (q, st_, sp_) in flags:
                        ins = e.matmul(acc[:, q, :], lhsT=pT[:, pslot, q * 128:(q + 1) * 128],
                                       rhs=VX[:, vb, h, 0:65], start=st_, stop=sp_)
                    return ins
                S.add("pe", fpv, reads=[("pT", pslot_), ("VX", vb_, h_ // 4)], writes=[("ps", ab_)])
                if last_:
                    normalise(h_, ab_)

    def attn_norm_T(nb, P):
        for bi in range(nb):
            pw = 128 if bi < nb - 1 else P
            ys_ = bi % 2
            rms_rows(attn[0:pw, bi, :], pw, 512, ya[0:pw, ys_, :], [("attn", bi)], [("ya", ys_)])
            pview = ps[6][:, :].bitcast(BF16).rearrange("p (c t) -> p c t", c=8)

            def ft(e, ys_=ys_, pw=pw):
                for c in range(4):
                    ins = e.transpose(out=pview[:, c, 0:pw], in_=ya[0:pw, ys_, c * 128:(c + 1) * 128], identity=ident[:pw, :pw])
                return ins
            S.add("pe", ft, reads=[("ya", ys_)], writes=[("ps", 6)])

            def fc(e, bi=bi, pw=pw):
                gsl = gcol[:, 24:28].unsqueeze(2).to_broadcast([128, 4, pw])
                return e.tensor_tensor(out=yaT[:, :, bi * 128:bi * 128 + pw], in0=pview[:, 0:4, 0:pw], in1=gsl, op=ALU.mult)
            S.add("dve", fc, reads=[("ps", 6)], writes=[("yaT", bi)])

    def out_proj(slots, P, rc_col, rc_res, gbase_next):
        nb = len(slots)
        sls = [wload(wo[dq], "cvMo") for dq in range(4)]
        for bi, slot in enumerate(slots):
            pw = 128 if bi < nb - 1 else P
            for dq in range(4):
                sl = sls[dq]
                bk = big_bank()

                def fm(e, sl=sl, bk=bk, bi=bi, pw=pw):
                    for c in range(4):
                        e.matmul(ps[bk][0:pw, 0:256], lhsT=ycgT[:, c, bi * 128:bi * 128 + pw],
                                 rhs=wring[:, sl, c * 256:(c + 1) * 256], start=(c == 0), stop=(c == 3))
                    for c in range(4):
                        ins = e.matmul(ps[bk][0:pw, 256:512], lhsT=yaT[:, c, bi * 128:bi * 128 + pw],
                                       rhs=wring[:, sl, (4 + c) * 256:(5 + c) * 256], start=(c == 0), stop=(c == 3))
                    return ins
                S.add("pe", fm, reads=[("wr", sl), ("yaT", bi)] + [("ycgT", c) for c in range(4)], writes=[("ps", bk)])

                def fr(e, bk=bk, bi=bi, slot=slot, pw=pw, dq=dq):
                    xsl = xres[0:pw, slot, dq * 256:(dq + 1) * 256]
                    return e.scalar_tensor_tensor(out=xsl, in0=ps[bk][0:pw, 0:256], scalar=st[0:pw, rc_col + bi:rc_col + bi + 1],
                                                  in1=xsl, op0=ALU.mult, op1=ALU.add)
                S.add("dve", fr, reads=[("ps", bk), ("x", slot), rc_res], writes=[("x", slot)])

                def fr2(e, bk=bk, bi=bi, slot=slot, pw=pw, dq=dq):
                    xsl = xres[0:pw, slot, dq * 256:(dq + 1) * 256]
                    return e.tensor_tensor(out=xsl, in0=xsl, in1=ps[bk][0:pw, 256:512], op=ALU.add)
                S.add("dve", fr2, reads=[("ps", bk), ("x", slot)], writes=[("x", slot)])
            normA(bi, slot, pw)
            if bi >= 1:
                normB(bi - 1, 128, gbase_next)
        normB(nb - 1, 128 if nb > 1 else P, gbase_next)

    ost = [0]

    def final_norm_out(slots, P, dst_of, after_block=None):
        nb = len(slots)
        for bi, slot in enumerate(slots):
            pw = 128 if bi < nb - 1 else P
            c = stat_cols(3)
            m_ap, l_ap, r_ap = st[:pw, c:c + 1], st[:pw, c + 1:c + 2], st[:pw, c + 2:c + 3]
            sres = ("st", c)
            src = xres[0:pw, slot, :]

            def f1(e, src=src, m_ap=m_ap, pw=pw):
                return e.activation(out=junk[:pw, :], in_=src, func=AF.Square, scale=1.0 / 32, accum_out=m_ap)
            S.add("act", f1, reads=[("x", slot)], writes=[sres])

            def f2(e, m_ap=m_ap, l_ap=l_ap, pw=pw):
                return e.activation(out=l_ap, in_=m_ap, func=AF.Ln, bias=epsc[:pw, 0:1], scale=1.0)
            S.add("act", f2, reads=[sres], writes=[sres])

            def f3(e, l_ap=l_ap, r_ap=r_ap):
                return e.activation(out=r_ap, in_=l_ap, func=AF.Exp, scale=-0.5)
            S.add("act", f3, reads=[sres], writes=[sres])
            os_ = ost[0] % 2
            ost[0] += 1
            odst = attn[0:pw, 2 * os_:2 * os_ + 2, :].rearrange("p a b -> p (a b)")

            def f4(e, src=src, r_ap=r_ap, odst=odst, pw=pw):
                return e.scalar_tensor_tensor(out=odst, in0=src, scalar=r_ap, in1=gfb[0:pw, :], op0=ALU.mult, op1=ALU.mult)
            S.add("dve", f4, reads=[sres, ("x", slot)], writes=[("attn", 2 * os_), ("attn", 2 * os_ + 1)])
            dst = dst_of(bi)

            def fd(e, dst=dst, odst=odst, os_=os_):
                return e.dma_start(out=dst, in_=odst).then_inc(SEMS["yout%d" % os_], 16)
            S.add("pool", fd, reads=[("attn", 2 * os_), ("attn", 2 * os_ + 1)], dma_sem="yout%d" % os_)
            if after_block is not None:
                after_block(bi)

    cc_count = [0]

    def cache_copy_some(n):
        if not (cfg.do_sample and cfg.do_cachecopy):
            return
        total = NS * 2
        for _ in range(n):
            i = cc_count[0]
            if i >= total:
                return
            cc_count[0] += 1
            b, which = i // 2, i % 2
            src = (ck, cv)[which]
            dst = (cks, cvs)[which]
            def fn(e, b=b, src=src, dst=dst):
                e.dma_start(out=dst[b, 0:2032, :].rearrange("(a r) c -> a (r c)", a=16),
                            in_=src[b, 1:2033, :].rearrange("(a r) c -> a (r c)", a=16)).then_inc(SEMS["cc"], 16)
                return e.dma_start(out=dst[b, 2032:2047, :], in_=src[b, 2033:2048, :]).then_inc(SEMS["cc"], 16)
            S.add("pool", fn, dma_sem="cc", ndma=2)

    prenormed = [False]
    for n_t, (seq_i, ti) in enumerate(tiles):
        slots = pending_slots
        tok0 = ti * T
        cs_slot = n_t % 2

        def fcs(e, cs_slot=cs_slot, tok0=tok0):
            return e.dma_start(out=cs[:, cs_slot, :, :], in_=csp_d[tok0:tok0 + T, :].rearrange("(b p) c -> p b c", p=128)
                               ).then_inc(SEMS["cs%d" % cs_slot], 16)
        S.add("sp", fcs, writes=[("cs", cs_slot)], dma_sem="cs%d" % cs_slot)

        STG = cfg.stage
        if STG >= 1 and not prenormed[0]:
            norm_to_hT(slots, 128, 0)
        prenormed[0] = False
        if STG >= 2:
            ffn(0, slots, 128)
        next_slots = []
        if n_t + 1 < len(tiles):
            xload(tiles[n_t + 1][0], tiles[n_t + 1][1], (0, 1), next_slots)
        want_out = (tok0 >= SEQ - KEEP)
        ring = ti % 5
        if STG >= 3:
            norm_to_hT(slots, 128, 8)
            conv_path(slots, 128, ti == 0, False)
            rc_col, rc_res = conv_rstd(4, 128)
        if STG >= 4:
            vproj(slots, 128, lambda bi: ring * 4 + bi, False, seq_i, tok0, want_out)
        if STG >= 5:
            for g in range(4):
                qkproj(slots, 128, g, cs_slot, ring * 512, False)
                if g == 0:
                    cache_copy_some(2)
                if g >= 1 and STG >= 6:
                    attention_prompt(ti, g - 1)
            if STG >= 6:
                attention_prompt(ti, 3)
            if want_out:
                r0 = tok0 - (SEQ - KEEP)

                def fko(e, r0=r0, seq_i=seq_i):
                    return e.dma_start(out=ckp[seq_i, r0:r0 + T, :].rearrange("(b p) c -> p b c", p=128), in_=kout[:, :, :]
                                       ).then_inc(SEMS["kout"], 16)
                S.add("pool", fko, reads=[("kout", bi) for bi in range(4)], dma_sem="kout")
        if ti == NT - 1 and STG >= 3:
            def fsc(e, seq_i=seq_i):
                ins = None
                for j in range(4):
                    ins = e.dma_start(out=scp[seq_i, :, j * 128:(j + 1) * 128].rearrange("t p -> p t"),
                                      in_=ubuf[:, j, 512:514], allow_slow_non_contiguous=True).then_inc(SEMS["scp"], 16)
                return ins
            S.add("pool", fsc, reads=[("u", j) for j in range(4)], dma_sem="scp", ndma=4)
        if STG >= 7:
            attn_norm_T(4, 128)
            out_proj(slots, 128, rc_col, rc_res, 16)
        if STG >= 8:
            ffn(1, slots, 128)
        def after_blk(bdone, n_t=n_t, next_slots=next_slots):
            if n_t + 1 >= len(tiles):
                return
            if bdone in (0, 1):
                xload(tiles[n_t + 1][0], tiles[n_t + 1][1], (2 + bdone,), next_slots)
            if STG >= 8 and bdone == 1:
                for b_ in (0, 1):
                    normA(b_, next_slots[b_], 128)
                    normB(b_, 128, 0)
            if STG >= 8 and bdone == 3:
                for b_ in (2, 3):
                    normA(b_, next_slots[b_], 128)
                    normB(b_, 128, 0)
                prenormed[0] = True
        final_norm_out(slots, 128, lambda bi, seq_i=seq_i, tok0=tok0: yp[seq_i, tok0 + bi * 128:tok0 + (bi + 1) * 128, :],
                       after_block=after_blk)
        pending_slots = next_slots


    if cfg.do_sample:
        P = NS
        sslot = xslot_ctr[0] % 6
        xslot_ctr[0] += 1

        def fxs(e):
            return e.dma_start(out=xres[0:P, sslot, :], in_=xs[:, :]).then_inc(SEMS["x%d" % sslot], 16)
        S.add("pool", fxs, writes=[("x", sslot)], dma_sem="x%d" % sslot)
        slots = [sslot]
        norm_to_hT(slots, P, 0)
        ffn(0, slots, P)
        norm_to_hT(slots, P, 8)
        for j in range(4):
            def fst_(e, j=j):
                e.dma_start(out=ubuf[:, j, 128:128 + P], in_=sconv[:, 0, j * 128:(j + 1) * 128].rearrange("b p -> p b"),
                            allow_slow_non_contiguous=True).then_inc(SEMS["sst%d" % j], 16)
                return e.dma_start(out=ubuf[:, j, 256:256 + P], in_=sconv[:, 1, j * 128:(j + 1) * 128].rearrange("b p -> p b"),
                                   allow_slow_non_contiguous=True).then_inc(SEMS["sst%d" % j], 16)
            S.add("sp", fst_, reads=[("u", j)], writes=[("sst", j), ("u", j)], dma_sem="sst%d" % j, ndma=2)
        conv_path(slots, P, False, True)
        rc_col, rc_res = conv_rstd(1, P)
        def fsc0(e):
            return e.dma_start(out=scs[:, 0, :], in_=sconv[:, 1, :]).then_inc(SEMS["snew"], 16)
        S.add("pool", fsc0, dma_sem="snew")

        def fsc1(e):
            ins = None
            for j in range(4):
                ins = e.dma_start(out=scs[:, 1, j * 128:(j + 1) * 128].rearrange("b p -> p b"), in_=ubuf[:, j, 2:2 + P],
                                  allow_slow_non_contiguous=True).then_inc(SEMS["snew"], 16)
            return ins
        S.add("pool", fsc1, reads=[("u", j) for j in range(4)], dma_sem="snew", ndma=4)
        vproj(slots, P, None, True, 0, 0, False)
        for g in range(4):
            qkproj(slots, P, g, 0, 0, True)

        def fkn(e):
            return e.dma_start(out=cks[:, WBUF - 1, :], in_=kout[0:P, 0, :]).then_inc(SEMS["snew"], 16)
        S.add("pool", fkn, reads=[("kout", 0)], dma_sem="snew")

        gres = [("gT", fc) for fc in range(NFC)]
        gflat = gT[:, :, :].rearrange("p a b -> p (a b)")
        f32v = lambda a, n: gflat[:, a:a + 2 * n].bitcast(F32)
        Ks = [f32v(0, 512), f32v(1024, 512)]
        Vs = [f32v(2048, 512), f32v(3072, 512)]
        prods = [f32v(4096, 512), f32v(10240, 512)]
        Wbs = [gflat[:, 5120:5632], gflat[:, 6656:7168]]
        qbf = gflat[0:P, 5632:6144]
        scrs = [f32v(6144, 64), f32v(6272, 64)]
        pbfs = [gflat[:, 6400:6408], gflat[:, 6408:6416]]
        nsm = f32v(0, 600)

        def fqb(e):
            return e.tensor_copy(out=qbf, in_=qtok[0:P, :])
        S.add("dve", fqb, reads=[("qtok", g) for g in range(4)] + gres, writes=gres + ["qbf"])
        pats = [(128, 1), (512, 4), (2048, 16)]
        n_it = 0
        pend_sh = []
        for b in range(P):
            bkq = big_bank()

            def fqbc(e, b=b, bkq=bkq):
                return e.matmul(ps[bkq][:, :], lhsT=ident[0:P, b:b + 1].to_broadcast([P, 128]), rhs=qbf,
                                start=True, stop=True)
            S.add("pe", fqbc, reads=["qbf"], writes=[("ps", bkq)])
            for pi, (w_, d_) in enumerate(pats):
                sl_ = n_it % 2
                start = WBUF - 128 * d_

                def fld(e, b=b, sl_=sl_, start=start, d_=d_):
                    e.dma_start(out=Ks[sl_], in_=ck[b, start:WBUF:d_, :]).then_inc(SEMS["ks%d" % sl_], 16)
                    return e.dma_start(out=Vs[sl_], in_=cv[b, start:WBUF:d_, :]).then_inc(SEMS["ks%d" % sl_], 16)
                S.add("sp", fld, writes=[("ksv", sl_)], dma_sem="ks%d" % sl_, ndma=2)

                prod, scr, Wb, pbf = prods[sl_], scrs[sl_], Wbs[sl_], pbfs[sl_]

                def fpr(e, sl_=sl_, bkq=bkq, prod=prod):
                    return e.tensor_tensor(out=prod, in0=Ks[sl_], in1=ps[bkq][:, :], op=ALU.mult)
                S.add("dve", fpr, reads=[("ksv", sl_), ("ps", bkq)], writes=[("prod", sl_)])

                def frd(e, prod=prod, scr=scr):
                    return e.tensor_reduce(out=scr[:, 0:8], in_=prod.rearrange("p (h d) -> p h d", h=8), axis=AX.X, op=ALU.add)
                S.add("dve", frd, reads=[("prod", sl_)], writes=[("scr0", sl_)])

                def fex(e, scr=scr):
                    return e.activation(out=scr[:, 8:16], in_=scr[:, 0:8], func=AF.Exp, scale=0.125)
                S.add("act", fex, reads=[("scr0", sl_)], writes=[("scr1", sl_)])

                def second_half(sl_=sl_, scr=scr, Wb=Wb, pbf=pbf, b=b, n_it_=n_it):
                    def fw(e):
                        return e.tensor_tensor(out=Wb.rearrange("p (h d) -> p h d", h=8), in0=Vs[sl_].rearrange("p (h d) -> p h d", h=8),
                                               in1=scr[:, 8:16].unsqueeze(2).to_broadcast([128, 8, 64]), op=ALU.mult)
                    S.add("dve", fw, reads=[("ksv", sl_), ("scr1", sl_)], writes=[("Wb", sl_)])

                    def fpb(e):
                        return e.tensor_copy(out=pbf, in_=scr[:, 8:16])
                    S.add("dve", fpb, reads=[("scr1", sl_)], writes=[("pbf", sl_)])
                    first = (n_it_ == 0)
                    last = (n_it_ == P * 3 - 1)

                    def fac(e):
                        e.matmul(ps[4][0:P, :], lhsT=zsel[:, 15 - b:15 - b + P], rhs=Wb, start=first, stop=last)
                        return e.matmul(ps[5][0:P, 0:8], lhsT=zsel[:, 15 - b:15 - b + P], rhs=pbf, start=first, stop=last)
                    S.add("pe", fac, reads=[("Wb", sl_), ("pbf", sl_)], writes=[("ps", 4), ("ps", 5)])
                if pend_sh:
                    pend_sh.pop(0)()
                pend_sh.append(second_half)
                n_it += 1
        for sh_ in pend_sh:
            sh_()
        nprod = nsm[0:P, 0:512]
        ns_s = nsm[0:P, 512:520]
        ns_p = nsm[0:P, 520:528]
        ns_d = nsm[0:P, 528:536]
        ns_r = nsm[0:P, 536:544]

        def fn_a(e):
            return e.tensor_tensor(out=nprod, in0=qtok[0:P, :], in1=kout[0:P, 0, :], op=ALU.mult)
        S.add("dve", fn_a, reads=[("qtok", g) for g in range(4)] + [("kout", 0)] + gres, writes=["nprod", ("ksv", 0), ("ksv", 1)])

        def fn_b(e):
            return e.tensor_reduce(out=ns_s, in_=nprod.rearrange("p (h d) -> p h d", h=8), axis=AX.X, op=ALU.add)
        S.add("dve", fn_b, reads=["nprod"], writes=["ns_s"])

        def fn_c(e):
            return e.activation(out=ns_p, in_=ns_s, func=AF.Exp, scale=0.125)
        S.add("act", fn_c, reads=["ns_s"], writes=["ns_p"])

        def fn_d(e):
            return e.tensor_tensor(out=nprod.rearrange("p (h d) -> p h d", h=8), in0=vnew[0:P, :].rearrange("p (h d) -> p h d", h=8),
                                   in1=ns_p.unsqueeze(2).to_broadcast([P, 8, 64]), op=ALU.mult)
        S.add("dve", fn_d, reads=["ns_p", ("vnew", 0), ("vnew", 1), "nprod"] + gres, writes=["nprod"])

        def fn_e(e):
            return e.scalar_tensor_tensor(out=nprod, in0=nprod, scalar=3.0, in1=ps[4][0:P, :], op0=ALU.mult, op1=ALU.add)
        S.add("dve", fn_e, reads=["nprod", ("ps", 4)], writes=["nprod"])

        def fn_f(e):
            return e.scalar_tensor_tensor(out=ns_d, in0=ns_p, scalar=3.0, in1=ps[5][0:P, 0:8], op0=ALU.mult, op1=ALU.add)
        S.add("dve", fn_f, reads=["ns_p", ("ps", 5)], writes=["ns_d"])

        def fn_g(e):
            return e.reciprocal(out=ns_r, in_=ns_d)
        S.add("dve", fn_g, reads=["ns_d"], writes=["ns_r"])

        def fn_h(e):
            return e.tensor_tensor(out=attn[0:P, 0, :].rearrange("p (h d) -> p h d", h=8), in0=nprod.rearrange("p (h d) -> p h d", h=8),
                                   in1=ns_r.unsqueeze(2).to_broadcast([P, 8, 64]), op=ALU.mult)
        S.add("dve", fn_h, reads=["ns_r", "nprod"], writes=[("attn", 0), ("attn", 1), ("attn", 2), ("attn", 3)] + gres)
        attn_norm_T(1, P)
        out_proj(slots, P, rc_col, rc_res, 16)
        ffn(1, slots, P)
        final_norm_out(slots, P, lambda bi: ys[:, :])

    cache_copy_some(1000)

    S.finalize()
    for k in S.sem_keys() + ["init"]:
        SEMS[k] = es.enter_context(nc.semaphore(k))
    finals = [(k, v) for k, v in S.dma_cnt.items()]
    const_total = S.dma_cnt.get("const", 0)
    with nc.Block() as block:
        @block.sync
        def _(e):
            S.emit("sp", e, SEMS)

        @block.gpsimd
        def _(e):
            S.emit("pool", e, SEMS, extra_final_waits=finals)

        @block.scalar
        def _(e):
            e.wait_ge(SEMS["const"], const_total)
            e.wait_ge(SEMS["init"], 1)
            S.emit("act", e, SEMS)

        @block.vector
        def _(e):
            init_fn(e).then_inc(SEMS["init"], 1)
            e.wait_ge(SEMS["const"], const_total)
            e.wait_ge(SEMS["init"], 1)
            S.emit("dve", e, SEMS)

        @block.tensor
        def _(e):
            e.wait_ge(SEMS["const"], const_total)
            e.wait_ge(SEMS["init"], 1)
            S.emit("pe", e, SEMS)
    es.close()
    return nc


def _consts(seq):
    ident = np.eye(128, dtype=np.float32)
    k = np.arange(128)[:, None]
    j = np.arange(GW)[None, :]
    dist = j - 384 - k
    f = np.zeros((128, GW), np.float32)
    for (w, d) in PATTERNS:
        f += ((dist >= 0) & (dist % d == 0) & (dist // d <= w // d)).astype(np.float32)
    half = 32
    inv = (10000.0 ** (-np.arange(half, dtype=np.float32) * 2.0 / 64)).astype(np.float32)
    pos = np.arange(seq, dtype=np.float32)
    ang = (pos[:, None] * inv[None, :]).astype(np.float32)
    csp = np.concatenate([np.cos(ang), np.sin(ang)], axis=1).astype(np.float32)
    angs = (np.float32(PAST_LEN) * inv)[None, :].astype(np.float32)
    css = np.concatenate([np.cos(angs), np.sin(angs)], axis=1).astype(np.float32)
    return dict(ident=ident.astype(ml_dtypes.bfloat16), identf=ident, gmask=f.astype(ml_dtypes.bfloat16),
                csp=csp, css=css)


def make_in_maps(cfg, ncores, inp):
    c = _consts(cfg.seq)
    g = lambda a: np.asarray(a, np.float32)
    gvec = np.zeros((128, 32), np.float32)
    gvec[:, 0:8] = g(inp["norm_ffn1"])[0].reshape(8, 128).T
    gvec[:, 8:16] = g(inp["norm_mix"])[0].reshape(8, 128).T
    gvec[:, 16:24] = g(inp["norm_ffn2"])[0].reshape(8, 128).T
    gvec[:, 24:28] = g(inp["out_norm_attn"])[0].reshape(4, 128).T
    gvec[:, 28:32] = g(inp["out_norm_conv"])[0].reshape(4, 128).T
    cw = g(inp["conv_w"])[0]
    cwv = np.ascontiguousarray(cw.reshape(3, 4, 128).transpose(2, 1, 0).reshape(128, 12))
    shared = dict(
        w1a=g(inp["ffn1_w1"])[0], w3a=g(inp["ffn1_w3"])[0], w2a=g(inp["ffn1_w2"])[0],
        w1b=g(inp["ffn2_w1"])[0], w3b=g(inp["ffn2_w3"])[0], w2b=g(inp["ffn2_w2"])[0],
        win=g(inp["w_in"])[0], wout=g(inp["w_out"])[0], gvec=gvec, cwv=cwv,
        gfin=g(inp["norm_final"]).reshape(1, D), **c)
    xp = g(inp["x_prompt"])
    xs = g(inp["x_sample"]).reshape(-1, D)
    sc = g(inp["state_conv"])[0]
    ckf = g(inp["cache_k"])[0].reshape(-1, WBUF, 512)
    cvf = g(inp["cache_v"])[0].reshape(-1, WBUF, 512)
    maps = []
    for i in range(ncores):
        m = dict(shared)
        m["xp"] = np.ascontiguousarray(xp[i * cfg.nseq:(i + 1) * cfg.nseq])
        m["xs"] = np.ascontiguousarray(xs[i * cfg.ns:(i + 1) * cfg.ns])
        m["sconv"] = np.ascontiguousarray(sc[i * cfg.ns:(i + 1) * cfg.ns])
        m["ck"] = np.ascontiguousarray(ckf[i * cfg.ns:(i + 1) * cfg.ns])
        m["cv"] = np.ascontiguousarray(cvf[i * cfg.ns:(i + 1) * cfg.ns])
        maps.append(m)
    return maps


def gather(cfg, ncores, results):
    cat = lambda k: np.concatenate([np.asarray(r[k]) for r in results], axis=0)
    yp = cat("yp")
    ys = cat("ys").reshape(ncores * cfg.ns, 1, D)
    scp = cat("scp")[None]
    scs = cat("scs")[None]
    ckp = cat("ckp").reshape(ncores * cfg.nseq, cfg.keep, 8, 64)[None]
    cvp = cat("cvp").reshape(ncores * cfg.nseq, cfg.keep, 8, 64)[None]
    cks = cat("cks").reshape(ncores * cfg.ns, WBUF, 8, 64)[None]
    cvs = cat("cvs").reshape(ncores * cfg.ns, WBUF, 8, 64)[None]
    return (yp, ys, scp, scs, ckp, cvp, cks, cvs)


def kernel(**inputs):
    cfg = Cfg()
    ncores = 8
    nc = build_program(cfg)
    maps = make_in_maps(cfg, ncores, inputs)
    res = run_bass_kernel_spmd(nc, maps, core_ids=list(range(ncores)))
    return gather(cfg, ncores, res.results)
```

```python
from contextlib import ExitStack
import numpy as np
import ml_dtypes
import concourse.bass as bass
import concourse.mybir as mybir
from concourse.bass_utils import run_bass_kernel_spmd

F32 = mybir.dt.float32
BF16 = mybir.dt.bfloat16
AF = mybir.ActivationFunctionType
ALU = mybir.AluOpType
AX = mybir.AxisListType

D = 1024
DFF = 2816
NFC = 22
NPAIR = 11
T = 512
EPS = 1e-6
PATTERNS = ((128, 1), (512, 4), (2048, 16))
WBUF = 2048
PAST_LEN = 8192
GW = 2944
VW = 66


class Cfg:
    def __init__(self, nseq=2, seq=4096, ns=16, do_sample=True, do_cachecopy=True, stage=99):
        self.stage = stage
        self.nseq = nseq
        self.seq = seq
        self.ns = ns
        self.nt = seq // T
        self.keep = min(WBUF, seq)
        self.do_sample = do_sample
        self.do_cachecopy = do_cachecopy


class _Op:
    __slots__ = ("eng", "fn", "deps", "sig", "tok", "dma", "idx")


class _Res:
    __slots__ = ("writer", "readers")

    def __init__(self):
        self.writer = None
        self.readers = {}


class Sched:
    ENGS = ("pe", "act", "dve", "pool", "sp")

    def __init__(self):
        self.ops = {e: [] for e in self.ENGS}
        self.res = {}
        self.dma_cnt = {}

    def add(self, eng, fn, reads=(), writes=(), dma_sem=None, ndma=1):
        op = _Op()
        op.eng = eng
        op.fn = fn
        op.sig = False
        op.dma = dma_sem is not None
        op.tok = None
        psr = [r for r in reads if isinstance(r, tuple) and r[0] == "ps"]
        if psr:
            writes = list(writes) + [r for r in psr if r not in writes]
        deps = {}
        for r in reads:
            rs = self.res.get(r)
            if rs is not None and rs.writer is not None:
                deps[id(rs.writer)] = rs.writer
        for w in writes:
            rs = self.res.get(w)
            if rs is not None:
                if rs.writer is not None:
                    deps[id(rs.writer)] = rs.writer
                for rd in rs.readers.values():
                    deps[id(rd)] = rd
        for r in reads:
            rs = self.res.setdefault(r, _Res())
            key = dma_sem if op.dma else eng
            rs.readers[key] = op
        for w in writes:
            rs = self.res.setdefault(w, _Res())
            rs.writer = op
            rs.readers = {}
        op.deps = []
        for d in deps.values():
            if d is op:
                continue
            if (not d.dma) and d.eng == eng and eng == "pe":
                continue
            d.sig = True
            op.deps.append(d)
        if op.dma:
            c = self.dma_cnt.get(dma_sem, 0) + 16 * ndma
            self.dma_cnt[dma_sem] = c
            op.tok = (dma_sem, c)
        self.ops[eng].append(op)
        return op

    def finalize(self):
        for e in self.ENGS:
            c = 0
            for op in self.ops[e]:
                if not op.dma and op.sig:
                    c += 1
                    op.tok = ("eng_" + e, c)

    def sem_keys(self):
        keys = ["eng_" + e for e in self.ENGS]
        keys += list(self.dma_cnt.keys())
        return keys

    def emit(self, eng_name, eng, sems, extra_final_waits=()):
        waited = {}
        for op in self.ops[eng_name]:
            need = {}
            for d in op.deps:
                k, v = d.tok
                if waited.get(k, 0) >= v:
                    continue
                if need.get(k, 0) < v:
                    need[k] = v
            for k, v in need.items():
                eng.wait_ge(sems[k], v)
                waited[k] = v
            last = op.fn(eng)
            if not op.dma and op.sig:
                last.then_inc(sems["eng_" + eng_name], 1)
        for k, v in extra_final_waits:
            if waited.get(k, 0) < v:
                eng.wait_ge(sems[k], v)
                waited[k] = v


def build_program(cfg):
    nc = bass.Bass("TRN2", target_bir_lowering=False)
    S = Sched()
    NSEQ, SEQ, NS, NT = cfg.nseq, cfg.seq, cfg.ns, cfg.nt
    KEEP = cfg.keep

    def din(name, shape, dt=F32):
        return nc.dram_tensor(name, list(shape), dt, kind="ExternalInput").ap()

    def dout(name, shape, dt=F32):
        return nc.dram_tensor(name, list(shape), dt, kind="ExternalOutput").ap()

    def dscr(name, shape, dt=BF16):
        return nc.dram_tensor(name, list(shape), dt, kind="Internal").ap()

    xp = din("xp", [NSEQ, SEQ, D])
    xs = din("xs", [NS, D])
    sconv = din("sconv", [NS, 2, 512])
    ck = din("ck", [NS, WBUF, 512])
    cv = din("cv", [NS, WBUF, 512])
    w1 = [din("w1a", [D, DFF]), din("w1b", [D, DFF])]
    w3 = [din("w3a", [D, DFF]), din("w3b", [D, DFF])]
    w2 = [din("w2a", [DFF, D]), din("w2b", [DFF, D])]
    win = din("win", [D, 3072])
    wout = din("wout", [D, D])
    gvec = din("gvec", [128, 32])
    cwv = din("cwv", [128, 12])
    gfin = din("gfin", [1, D])
    ident_d = din("ident", [128, 128], BF16)
    gmask_d = din("gmask", [128, GW], BF16)
    csp_d = din("csp", [SEQ, 64])
    css_d = din("css", [1, 64])
    identf_d = din("identf", [128, 128])

    yp = dout("yp", [NSEQ, SEQ, D])
    ys = dout("ys", [NS, D])
    scp = dout("scp", [NSEQ, 2, 512])
    scs = dout("scs", [NS, 2, 512])
    ckp = dout("ckp", [NSEQ, KEEP, 512])
    cvp = dout("cvp", [NSEQ, KEEP, 512])
    cks = dout("cks", [NS, WBUF, 512])
    cvs = dout("cvs", [NS, WBUF, 512])

    ffA = [dscr("ffA0", [NPAIR, 2, 128, 2048]), dscr("ffA1", [NPAIR, 2, 128, 2048])]
    ffB = [dscr("ffB0", [NPAIR, 128, 2048]), dscr("ffB1", [NPAIR, 128, 2048])]
    wcv = dscr("wcv", [6, 128, 2048])
    wv = dscr("wv", [2, 128, 2048])
    wqk = dscr("wqk", [4, 128, 2048])
    wo = dscr("wo", [4, 128, 2048])

    es = ExitStack()

    def sb(name, shape, dt):
        return es.enter_context(nc.sbuf_tensor(name, list(shape), dt))

    wring = sb("wring", [128, 6, 2048], BF16)
    xres = sb("xres", [128, 6, 1024], F32)
    hb = sb("hb", [128, 2, 1024], BF16)
    hT = sb("hT", [128, 8, 512], BF16)
    gT = sb("gT", [128, NFC, 512], BF16)
    sa = sb("sa", [128, 2, 512], F32)
    KT = sb("KT", [128, 4, 2560], BF16)
    VX = sb("VX", [128, 20, 8, VW], BF16)
    QT = sb("QT", [128, 8, 512], BF16)
    G = sb("G", [128, GW], BF16)
    gfb = sb("gfb", [128, D], F32)
    cs = sb("cs", [128, 2, 4, 64], F32)
    ubuf = sb("ubuf", [128, 4, 514], F32)
    csb = sb("csb", [128, 2, 512], F32)
    cacc = sb("cacc", [128, 1, 512], F32)
    ycb = sb("ycb", [128, 1, 512], F32)
    sqb = sb("sqb", [128, 1, 512], BF16)
    ycgT = sb("ycgT", [128, 4, 512], BF16)
    rtA = sb("rtA", [128, 1, 512], F32)
    rtB = sb("rtB", [128, 1, 512], F32)
    kout = sb("kout", [128, 4, 512], F32)
    vout = sb("vout", [128, 2, 256], F32)
    qb = sb("qb", [128, 2, 2, 128], BF16)
    kb = sb("kb", [128, 2, 2, 128], BF16)
    pT = sb("pT", [128, 6, 512], BF16)
    attn = sb("attn", [128, 4, 512], F32)
    ya = sb("ya", [128, 2, 512], BF16)
    yaT = sb("yaT", [128, 4, 512], BF16)
    ident = sb("identb", [128, 128], BF16)
    gcol = sb("gcol", [128, 32], F32)
    cwc = sb("cwc", [128, 12], F32)
    ones = sb("ones", [128, 1], BF16)
    epsc = sb("epsc", [128, 1], F32)
    junk = sb("junk", [128, 1024], BF16)
    st = sb("st", [128, 128], F32)
    rec = sb("rec", [128, 2, 4], F32)
    css = sb("css_s", [128, 64], F32)
    gflat0 = gT[:, :, :].rearrange("p a b -> p (a b)")
    qtok = gflat0[0:16, 8192:9216].bitcast(F32)
    vnew = gflat0[0:16, 9216:10240].bitcast(F32)
    GRES = [("gT", fc_) for fc_ in range(NFC)]
    zsel = sb("zsel", [128, 32], BF16)

    ps = [es.enter_context(nc.psum_tensor("ps%d" % i, [128, 512], F32)) for i in range(8)]

    big_rr = [0]

    def big_bank():
        b = big_rr[0] % 4
        big_rr[0] += 1
        return b

    nconst = [0]

    def const_load(out_ap, in_ap):
        def fn(e):
            return e.dma_start(out=out_ap, in_=in_ap).then_inc(SEMS["const"], 16)
        S.add("sp", fn, dma_sem="const")
        nconst[0] += 1

    SEMS = {}

    const_load(ident[:], ident_d[:, :])
    const_load(G[:], gmask_d[:, :])
    const_load(gcol[:], gvec[:, :])
    const_load(cwc[:], cwv[:, :])
    const_load(gfb[:].unsqueeze(1), gfin[0:1, :].partition_broadcast(128))
    const_load(css[:].unsqueeze(1), css_d[0:1, :].partition_broadcast(128))

    def init_fn(e):
        e.memset(ones[:], 1.0)
        e.memset(epsc[:], EPS)
        e.memset(ubuf[:], 0.0)
        e.memset(QT[:], 0.0)
        e.memset(zsel[:, 0:15], 0.0)
        e.memset(zsel[:, 16:32], 0.0)
        e.memset(zsel[:, 15:16], 1.0)
        return e.memset(VX[:], 1.0)

    xslot_ctr = [0]

    def xload(seq_i, ti, blocks, slots):
        for bi in blocks:
            slot = xslot_ctr[0] % 6
            xslot_ctr[0] += 1
            t0 = ti * T + bi * 128

            def fn(e, slot=slot, t0=t0):
                return e.dma_start(out=xres[:, slot, :], in_=xp[seq_i, t0:t0 + 128, :]).then_inc(SEMS["x%d" % slot], 16)
            S.add("pool", fn, writes=[("x", slot)], dma_sem="x%d" % slot)
            slots.append(slot)

    tiles = [(s_, ti_) for s_ in range(NSEQ) for ti_ in range(NT)]
    pending_slots = []
    xload(0, 0, (0, 1, 2, 3), pending_slots)

    conv_groups = {}

    def conv_dma(group, out_ap, in_ap):
        def fn(e):
            return e.dma_start(out=out_ap, in_=in_ap, max_dma_last_dim=8192).then_inc(SEMS[group], 16)
        S.add("pool", fn, writes=[group], dma_sem=group)

    def conv_ffA(l, pr, grp):
        for w_i, wsrc in enumerate((w1[l], w3[l])):
            conv_dma(grp,
                     ffA[l][pr, w_i].rearrange("p (dc c) -> p dc c", dc=8),
                     wsrc[:, pr * 256:(pr + 1) * 256].rearrange("(dc p) c -> p dc c", p=128))

    def conv_ffB(l, pr, grp):
        conv_dma(grp,
                 ffB[l][pr].rearrange("p (f d) -> p f d", f=2),
                 w2[l][pr * 256:(pr + 1) * 256, :].rearrange("(f p) d -> p f d", p=128))

    def grpA(l, pr):
        return "cvA%d_%d" % (l, pr // 4)

    for pr in range(NPAIR):
        conv_ffA(0, pr, grpA(0, pr))
    for pr in range(NPAIR):
        conv_ffB(0, pr, "cvB0")
    cc_cols = []
    for j in range(4):
        cc_cols += [512 + j * 128, 1024 + j * 128, j * 128]
    for i in range(6):
        for half in range(2):
            col = cc_cols[2 * i + half]
            conv_dma("cvMc",
                     wcv[i, :, half * 1024:(half + 1) * 1024].rearrange("p (dc c) -> p dc c", dc=8),
                     win[:, col:col + 128].rearrange("(dc p) c -> p dc c", p=128))
    for v in range(2):
        conv_dma("cvMv", wv[v].rearrange("p (dc c) -> p dc c", dc=8),
                 win[:, 2560 + v * 256:2560 + (v + 1) * 256].rearrange("(dc p) c -> p dc c", p=128))
    for g in range(4):
        for part, base in enumerate((1536, 2048)):
            conv_dma("cvMq",
                     wqk[g].rearrange("p (dc c) -> p dc c", dc=8)[:, :, part * 128:(part + 1) * 128],
                     win[:, base + g * 128:base + (g + 1) * 128].rearrange("(dc p) c -> p dc c", p=128))
    for dq in range(4):
        conv_dma("cvMo", wo[dq].rearrange("p (c8 c) -> p c8 c", c8=8),
                 wout[:, dq * 256:(dq + 1) * 256].rearrange("(c8 p) c -> p c8 c", p=128))
    for pr in range(NPAIR):
        conv_ffA(1, pr, "cvA1_%d" % (pr // 4))
    for pr in range(NPAIR):
        conv_ffB(1, pr, "cvB1")

    wcount = [0]

    def wload(src_ap, grp):
        slot = wcount[0] % 6
        wcount[0] += 1
        rname = ("wr", slot)

        def fn(e):
            return e.dma_start(out=wring[:, slot, :], in_=src_ap).then_inc(SEMS["wr%d" % slot], 16)
        S.add("sp", fn, reads=[grp], writes=[rname], dma_sem="wr%d" % slot)
        return slot

    stat_rr = [0]

    def stat_cols(n):
        c = (stat_rr[0] % 16) * 8
        stat_rr[0] += 1
        return c

    def rms_rows(src_ap, P, width, dst_bf_ap, src_res, dst_res, extra_scale_out=None):
        c = stat_cols(3)
        m_ap = st[:P, c:c + 1]
        l_ap = st[:P, c + 1:c + 2]
        r_ap = st[:P, c + 2:c + 3]
        sres = ("st", c)

        def f1(e):
            return e.activation(out=junk[:P, 0:width], in_=src_ap, func=AF.Square,
                                scale=float(width) ** -0.5, accum_out=m_ap)
        S.add("act", f1, reads=list(src_res), writes=[sres])

        def f2(e):
            return e.activation(out=l_ap, in_=m_ap, func=AF.Ln, bias=epsc[:P, 0:1], scale=1.0)
        S.add("act", f2, reads=[sres], writes=[sres])

        def f3(e):
            return e.activation(out=r_ap, in_=l_ap, func=AF.Exp, scale=-0.5)
        S.add("act", f3, reads=[sres], writes=[sres])

        def f4(e):
            return e.activation(out=dst_bf_ap, in_=src_ap, func=AF.Copy, scale=r_ap)
        S.add("act", f4, reads=[sres] + list(src_res), writes=list(dst_res))
        return r_ap, sres

    def normA(bi, slot, P):
        hs = bi % 2
        rms_rows(xres[:P, slot, :], P, D, hb[:P, hs, :], [("x", slot)], [("hb", hs)])

    def normB(bi, P, gbase):
        hs = bi % 2

        def ft(e, hs=hs):
            for dc in range(8):
                ins = e.transpose(out=ps[6][:, :].bitcast(BF16).rearrange("p (c t) -> p c t", c=8)[:, dc, 0:P],
                                  in_=hb[:P, hs, dc * 128:(dc + 1) * 128], identity=ident[:P, :P])
            return ins
        S.add("pe", ft, reads=[("hb", hs)], writes=[("ps", 6)])

        def fc(e, bi=bi):
            src = ps[6][:, :].bitcast(BF16).rearrange("p (c t) -> p c t", c=8)[:, :, 0:P]
            gsl = gcol[:, gbase:gbase + 8].unsqueeze(2).to_broadcast([128, 8, P])
            return e.tensor_tensor(out=hT[:, :, bi * 128:bi * 128 + P], in0=src, in1=gsl, op=ALU.mult)
        S.add("dve", fc, reads=[("ps", 6)], writes=[("hT", bi)])

    def norm_to_hT(slots, P, gbase):
        nb = len(slots)
        for bi, slot in enumerate(slots):
            normA(bi, slot, 128 if bi < nb - 1 else P)
            normB(bi, 128 if bi < nb - 1 else P, gbase)

    def ffn(l, slots, P):
        nb = len(slots)
        TW = (nb - 1) * 128 + P
        hres = [("hT", bi) for bi in range(nb)]
        for pr in range(NPAIR):
            s1 = wload(ffA[l][pr, 0], grpA(l, pr))
            s3 = wload(ffA[l][pr, 1], grpA(l, pr))
            for f2 in range(2):
                fc = pr * 2 + f2
                ba = fc % 2
                bb = 2 + fc % 2

                def fm(e, s1=s1, s3=s3, f2=f2, ba=ba, bb=bb):
                    for dc in range(8):
                        e.matmul(ps[ba][:, 0:TW], lhsT=wring[:, s1, dc * 256 + f2 * 128: dc * 256 + f2 * 128 + 128],
                                 rhs=hT[:, dc, 0:TW], start=(dc == 0), stop=(dc == 7))
                    for dc in range(8):
                        ins = e.matmul(ps[bb][:, 0:TW], lhsT=wring[:, s3, dc * 256 + f2 * 128: dc * 256 + f2 * 128 + 128],
                                       rhs=hT[:, dc, 0:TW], start=(dc == 0), stop=(dc == 7))
                    return ins
                S.add("pe", fm, reads=[("wr", s1), ("wr", s3)] + hres, writes=[("ps", ba), ("ps", bb)])
                ss = fc % 2

                def fs(e, ba=ba, ss=ss):
                    import os as _os
                    return e.activation(out=sa[:, ss, 0:TW], in_=ps[ba][:, 0:TW], func=(AF.Copy if _os.environ.get("NOSILU") else AF.Silu))
                S.add("act", fs, reads=[("ps", ba)], writes=[("sa", ss)])

                def fg(e, bb=bb, ss=ss, fc=fc):
                    return e.tensor_tensor(out=gT[:, fc, 0:TW], in0=sa[:, ss, 0:TW], in1=ps[bb][:, 0:TW], op=ALU.mult)
                S.add("dve", fg, reads=[("sa", ss), ("ps", bb)], writes=[("gT", fc)])
        import os as _os
        if _os.environ.get("FFN_P1ONLY"):
            return
        accb = lambda bi, half: (int(_os.environ.get("P2_OFF", "0")) + bi * 2 + half) % 8
        for pr in range(NPAIR):
            s2 = wload(ffB[l][pr], "cvB%d" % l)

            def fm2(e, s2=s2, pr=pr):
                ins = None
                for f2 in range(2):
                    fc = pr * 2 + f2
                    for bi in range(min(nb, int(_os.environ.get("P2_NBI", "9")))):
                        pw = 128 if bi < nb - 1 else P
                        for half in range(int(_os.environ.get("P2_NHALF", "2"))):
                            ins = e.matmul(ps[accb(bi, half)][0:pw, :],
                                           lhsT=gT[:, fc, bi * 128:bi * 128 + pw],
                                           rhs=wring[:, s2, f2 * 1024 + half * 512: f2 * 1024 + half * 512 + 512],
                                           start=(fc == 0), stop=(fc == NFC - 1))
                return ins
            S.add("pe", fm2, reads=[("wr", s2), ("gT", 2 * pr), ("gT", 2 * pr + 1)],
                  writes=[("ps", accb(bi, h)) for bi in range(nb) for h in range(2)])
        for bi, slot in enumerate(slots):
            if _os.environ.get("P2_NORES"):
                break
            pw = 128 if bi < nb - 1 else P
            for half in range(2):
                def fr(e, bi=bi, slot=slot, half=half, pw=pw):
                    xsl = xres[0:pw, slot, half * 512:(half + 1) * 512]
                    return e.scalar_tensor_tensor(out=xsl, in0=ps[accb(bi, half)][0:pw, :], scalar=0.5, in1=xsl,
                                                  op0=ALU.mult, op1=ALU.add)
                S.add("dve", fr, reads=[("ps", accb(bi, half)), ("x", slot)], writes=[("x", slot)])

    def conv_path(slots, P, first_tile, sample):
        nb = len(slots)
        TW = (nb - 1) * 128 + P
        hres = [("hT", bi) for bi in range(nb)]
        banks = {}
        pend_fst = []
        for i in range(6):
            sl = wload(wcv[i], "cvMc")
            for half in range(2):
                cc = 2 * i + half
                j, kind = cc // 3, cc % 3
                bk = big_bank()
                banks[kind] = bk

                def fm(e, sl=sl, half=half, bk=bk):
                    for dc in range(8):
                        ins = e.matmul(ps[bk][:, 0:TW], lhsT=wring[:, sl, half * 1024 + dc * 128: half * 1024 + dc * 128 + 128],
                                       rhs=hT[:, dc, 0:TW], start=(dc == 0), stop=(dc == 7))
                    return ins
                S.add("pe", fm, reads=[("wr", sl)] + hres, writes=[("ps", bk)])
                if kind == 2 and pend_fst:
                    pf, pr_ = pend_fst.pop(0)
                    S.add("pe", pf, reads=pr_, writes=[("ps", 7)])
                s2 = j % 2
                s1_ = 0
                if kind == 0:
                    def fcopy(e, bk=bk, s2=s2):
                        return e.activation(out=csb[:, s2, 0:TW], in_=ps[bk][:, 0:TW], func=AF.Copy)
                    S.add("act", fcopy, reads=[("ps", bk)], writes=[("csb", s2)])
                elif kind == 1:
                    if not sample:
                        if first_tile:
                            def fh(e, j=j):
                                return e.memset(ubuf[:, j, 0:2], 0.0)
                        else:
                            def fh(e, j=j):
                                return e.tensor_copy(out=ubuf[:, j, 0:2], in_=ubuf[:, j, 512:514])
                        S.add("dve", fh, reads=[("u", j)], writes=[("u", j)])

                    def fu(e, bk=bk, s2=s2, j=j):
                        return e.tensor_tensor(out=ubuf[:, j, 2:2 + TW], in0=csb[:, s2, 0:TW], in1=ps[bk][:, 0:TW], op=ALU.mult)
                    S.add("dve", fu, reads=[("ps", bk), ("csb", s2), ("u", j)], writes=[("u", j)])
                    if not sample:
                        def fa0(e, s2=s2, j=j, s1_=s1_):
                            return e.activation(out=cacc[:, s1_, 0:TW], in_=ubuf[:, j, 0:TW], func=AF.Copy,
                                                scale=cwc[:, j * 3:j * 3 + 1])
                        S.add("act", fa0, reads=[("u", j)], writes=[("cacc", s1_)])

                        def fa1(e, s2=s2, j=j, s1_=s1_):
                            return e.scalar_tensor_tensor(out=cacc[:, s1_, 0:TW], in0=ubuf[:, j, 1:1 + TW],
                                                          scalar=cwc[:, j * 3 + 1:j * 3 + 2], in1=cacc[:, s1_, 0:TW],
                                                          op0=ALU.mult, op1=ALU.add)
                        S.add("dve", fa1, reads=[("u", j), ("cacc", s1_)], writes=[("cacc", s1_)])

                        def fa2(e, s2=s2, j=j, s1_=s1_):
                            return e.scalar_tensor_tensor(out=cacc[:, s1_, 0:TW], in0=ubuf[:, j, 2:2 + TW],
                                                          scalar=cwc[:, j * 3 + 2:j * 3 + 3], in1=cacc[:, s1_, 0:TW],
                                                          op0=ALU.mult, op1=ALU.add)
                        S.add("dve", fa2, reads=[("u", j), ("cacc", s1_)], writes=[("cacc", s1_)])
                    else:
                        def fa0(e, s2=s2, j=j, s1_=s1_):
                            return e.activation(out=cacc[:, s1_, 0:TW], in_=ubuf[:, j, 128:128 + TW], func=AF.Copy,
                                                scale=cwc[:, j * 3:j * 3 + 1])
                        S.add("act", fa0, reads=[("u", j), ("sst", j)], writes=[("cacc", s1_)])

                        def fa1(e, s2=s2, j=j, s1_=s1_):
                            return e.scalar_tensor_tensor(out=cacc[:, s1_, 0:TW], in0=ubuf[:, j, 256:256 + TW],
                                                          scalar=cwc[:, j * 3 + 1:j * 3 + 2], in1=cacc[:, s1_, 0:TW],
                                                          op0=ALU.mult, op1=ALU.add)
                        S.add("dve", fa1, reads=[("u", j), ("sst", j), ("cacc", s1_)], writes=[("cacc", s1_)])

                        def fa2(e, s2=s2, j=j, s1_=s1_):
                            return e.scalar_tensor_tensor(out=cacc[:, s1_, 0:TW], in0=ubuf[:, j, 2:2 + TW],
                                                          scalar=cwc[:, j * 3 + 2:j * 3 + 3], in1=cacc[:, s1_, 0:TW],
                                                          op0=ALU.mult, op1=ALU.add)
                        S.add("dve", fa2, reads=[("u", j), ("cacc", s1_)], writes=[("cacc", s1_)])
                else:
                    def fy(e, bk=bk, s2=s2, s1_=s1_):
                        return e.tensor_tensor(out=ycb[:, s1_, 0:TW], in0=cacc[:, s1_, 0:TW], in1=ps[bk][:, 0:TW], op=ALU.mult)
                    S.add("dve", fy, reads=[("ps", bk), ("cacc", s1_)], writes=[("ycb", s1_)])

                    def fyg(e, s2=s2, j=j, s1_=s1_):
                        return e.activation(out=ycgT[:, j, 0:TW], in_=ycb[:, s1_, 0:TW], func=AF.Copy,
                                            scale=gcol[:, 28 + j:29 + j])
                    S.add("act", fyg, reads=[("ycb", s1_)], writes=[("ycgT", j)])

                    def fsq(e, s2=s2, s1_=s1_):
                        return e.activation(out=sqb[:, s1_, 0:TW], in_=ycb[:, s1_, 0:TW], func=AF.Square)
                    S.add("act", fsq, reads=[("ycb", s1_)], writes=[("sqb", s1_)])

                    def fst(e, s2=s2, j=j, s1_=s1_):
                        ins = None
                        for bi in range(nb):
                            pw = 128 if bi < nb - 1 else P
                            ins = e.matmul(ps[7][0:pw, bi:bi + 1], lhsT=sqb[:, s1_, bi * 128:bi * 128 + pw],
                                           rhs=ones[:, 0:1], start=(j == 0 and bi == 0), stop=(j == 3 and bi == nb - 1))
                        return ins
                    pend_fst.append((fst, [("sqb", s1_)]))
        for pf, pr_ in pend_fst:
            S.add("pe", pf, reads=pr_, writes=[("ps", 7)])

    def conv_rstd(nb, P):
        c = stat_cols(8)
        PP = 128 if nb > 1 else P

        def f2(e):
            return e.activation(out=st[:PP, c:c + nb], in_=ps[7][:PP, 0:nb], func=AF.Ln, bias=epsc[:PP, 0:1], scale=1.0 / 512)
        S.add("act", f2, reads=[("ps", 7)], writes=[("st", c)])

        def f3(e):
            return e.activation(out=st[:PP, c + 4:c + 4 + nb], in_=st[:PP, c:c + nb], func=AF.Exp, scale=-0.5)
        S.add("act", f3, reads=[("st", c)], writes=[("st", c)])
        return c + 4, ("st", c)

    def rope_pair(src0, src1, cs_ap, P, nbk, outs):
        raise NotImplementedError

    def vproj(slots, P, vb_of, sample, seq_i, tile_tok0, want_out):
        nb = len(slots)
        for v in range(2):
            sl = wload(wv[v], "cvMv")
            bks = [big_bank(), big_bank()]
            for bp in range((nb + 1) // 2):
                bk = bks[bp]
                bis = [bi for bi in (2 * bp, 2 * bp + 1) if bi < nb]

                def fm(e, sl=sl, bk=bk, bis=bis):
                    ins = None
                    for bi in bis:
                        pw = 128 if bi < nb - 1 else P
                        for dc in range(8):
                            ins = e.matmul(ps[bk][0:pw, (bi % 2) * 256:(bi % 2) * 256 + 256],
                                           lhsT=hT[:, dc, bi * 128:bi * 128 + pw],
                                           rhs=wring[:, sl, dc * 256:(dc + 1) * 256], start=(dc == 0), stop=(dc == 7))
                    return ins
                S.add("pe", fm, reads=[("wr", sl)] + [("hT", bi) for bi in bis], writes=[("ps", bk)])
                import os as _os
                vmode = int(_os.environ.get("VMODE", "3"))
                for bi in bis:
                    if vmode < 1:
                        break
                    pw = 128 if bi < nb - 1 else P
                    src = ps[bk][0:pw, (bi % 2) * 256:(bi % 2) * 256 + 256]
                    if not sample:
                        vb = vb_of(bi)

                        def fv(e, src=src, vb=vb, v=v, pw=pw):
                            return e.activation(out=VX[0:pw, vb, 4 * v:4 * v + 4, 0:64],
                                                in_=src.rearrange("p (h d) -> p h d", h=4), func=AF.Copy)
                        S.add("act", fv, reads=[("ps", bk)], writes=[("VX", vb, v)])
                    if sample:
                        def fo(e, src=src, v=v, pw=pw):
                            return e.tensor_copy(out=vnew[0:pw, v * 256:(v + 1) * 256], in_=src)
                        S.add("dve", fo, reads=[("ps", bk)], writes=[("vnew", v)] + GRES)

                        def fd(e, v=v, pw=pw):
                            return e.dma_start(out=cvs[:, WBUF - 1, v * 256:(v + 1) * 256],
                                               in_=vnew[0:pw, v * 256:(v + 1) * 256]).then_inc(SEMS["snew"], 16)
                        S.add("pool", fd, reads=[("vnew", v)] + GRES, dma_sem="snew")
                    elif want_out and vmode >= 2:
                        vs = (v * 4 + bi) % 2

                        def fo(e, src=src, vs=vs, pw=pw):
                            return e.tensor_copy(out=vout[0:pw, vs, :], in_=src)
                        S.add("dve", fo, reads=[("ps", bk)], writes=[("vout", vs)])
                        if False:
                            pass
                        else:
                            r0 = tile_tok0 + bi * 128 - (SEQ - KEEP)
                            dst = cvp[seq_i, r0:r0 + 128, v * 256:(v + 1) * 256]

                        def fd(e, dst=dst, vs=vs, pw=pw):
                            return e.dma_start(out=dst, in_=vout[0:pw, vs, :]).then_inc(SEMS["vout%d" % vs], 16)
                        if vmode >= 3:
                            S.add(_os.environ.get("VQ", "pool"), fd, reads=[("vout", vs)], dma_sem="vout%d" % vs)

    qkslot = [0]

    def qkproj(slots, P, g, cs_slot, kt_col0, sample):
        nb = len(slots)
        sl = wload(wqk[g], "cvMq")
        bks = [big_bank(), big_bank()]
        for bp in range((nb + 1) // 2):
            bk = bks[bp]
            bis = [bi for bi in (2 * bp, 2 * bp + 1) if bi < nb]
            nbk = len(bis)
            PP = 128 if nb > 1 else P

            def fm(e, sl=sl, bk=bk, bis=bis):
                ins = None
                for bi in bis:
                    pw = 128 if bi < nb - 1 else P
                    for dc in range(8):
                        ins = e.matmul(ps[bk][0:pw, (bi % 2) * 256:(bi % 2) * 256 + 256],
                                       lhsT=hT[:, dc, bi * 128:bi * 128 + pw],
                                       rhs=wring[:, sl, dc * 256:(dc + 1) * 256], start=(dc == 0), stop=(dc == 7))
                return ins
            S.add("pe", fm, reads=[("wr", sl)] + [("hT", bi) for bi in bis], writes=[("ps", bk)])
            rs = 0
            qs = qkslot[0] % 2
            qkslot[0] += 1
            src = ps[bk][0:PP, 0:nbk * 256].rearrange("p (b n t f) -> p b n t f", b=nbk, n=4, t=2)
            srcf = ps[bk][0:PP, 0:nbk * 256].rearrange("p (b m f) -> p b m f", b=nbk, m=8)
            if sample:
                cos_b = css[0:PP, 0:32].unsqueeze(1).unsqueeze(1).to_broadcast([PP, nbk, 8, 32])
                sin_b = css[0:PP, 32:64].unsqueeze(1).unsqueeze(1).to_broadcast([PP, nbk, 4, 32])
            else:
                b0 = bis[0]
                cos_b = cs[0:PP, cs_slot, b0:b0 + nbk, 0:32].unsqueeze(2).to_broadcast([PP, nbk, 8, 32])
                sin_b = cs[0:PP, cs_slot, b0:b0 + nbk, 32:64].unsqueeze(2).to_broadcast([PP, nbk, 4, 32])
            tA = rtA[0:PP, rs, 0:nbk * 256].rearrange("p (b n t f) -> p b n t f", b=nbk, n=4, t=2)
            tAf = rtA[0:PP, rs, 0:nbk * 256].rearrange("p (b m f) -> p b m f", b=nbk, m=8)
            tB = rtB[0:PP, rs, 0:nbk * 256].rearrange("p (b n t f) -> p b n t f", b=nbk, n=4, t=2)

            def fr(e, src=src, srcf=srcf, cos_b=cos_b, sin_b=sin_b, tAf=tAf, tB=tB):
                e.tensor_tensor(out=tAf, in0=srcf, in1=cos_b, op=ALU.mult)
                e.tensor_tensor(out=tB[:, :, :, 0, :], in0=src[:, :, :, 1, :], in1=sin_b, op=ALU.mult)
                return e.tensor_tensor(out=tB[:, :, :, 1, :], in0=src[:, :, :, 0, :], in1=sin_b, op=ALU.mult)
            S.add("dve", fr, reads=[("ps", bk), ("cs", cs_slot)], writes=[("rt", rs)])
            if sample:
                qdst = qtok[0:PP, g * 128:(g + 1) * 128].unsqueeze(1).rearrange("p b (n t f) -> p b n t f", n=2, t=2)
            else:
                qdst = qb[0:PP, qs, 0:nbk, :].rearrange("p b (n t f) -> p b n t f", n=2, t=2)
            kdst = kout[0:PP, bis[0]:bis[0] + nbk, g * 128:(g + 1) * 128].rearrange("p b (n t f) -> p b n t f", n=2, t=2)

            def fo(e, tA=tA, tB=tB, qdst=qdst, kdst=kdst):
                e.tensor_tensor(out=qdst[:, :, :, 0, :], in0=tA[:, :, 0:2, 0, :], in1=tB[:, :, 0:2, 0, :], op=ALU.subtract)
                e.tensor_tensor(out=qdst[:, :, :, 1, :], in0=tA[:, :, 0:2, 1, :], in1=tB[:, :, 0:2, 1, :], op=ALU.add)
                e.tensor_tensor(out=kdst[:, :, :, 0, :], in0=tA[:, :, 2:4, 0, :], in1=tB[:, :, 2:4, 0, :], op=ALU.subtract)
                return e.tensor_tensor(out=kdst[:, :, :, 1, :], in0=tA[:, :, 2:4, 1, :], in1=tB[:, :, 2:4, 1, :], op=ALU.add)
            S.add("dve", fo, reads=[("rt", rs)], writes=[("qb", qs), ("qtok", g)] + [("kout", bi) for bi in bis] + (GRES if sample else []))
            if sample:
                continue

            def fkb(e, qs=qs, bis=bis, nbk=nbk):
                return e.activation(out=kb[0:PP, qs, 0:nbk, :], in_=kout[0:PP, bis[0]:bis[0] + nbk, g * 128:(g + 1) * 128],
                                    func=AF.Copy)
            S.add("act", fkb, reads=[("kout", bi) for bi in bis], writes=[("kb", qs)])
            pview = ps[6][:, :].bitcast(BF16)

            def ftr(e, qs=qs, bis=bis, nbk=nbk):
                ins = None
                for i2 in range(nbk):
                    pw = 128 if bis[i2] < nb - 1 else P
                    e.transpose(out=pview[:, i2 * 128:i2 * 128 + pw], in_=qb[0:pw, qs, i2, :], identity=ident[:pw, :pw])
                    ins = e.transpose(out=pview[:, 256 + i2 * 128:256 + i2 * 128 + pw], in_=kb[0:pw, qs, i2, :],
                                      identity=ident[:pw, :pw])
                return ins
            S.add("pe", ftr, reads=[("qb", qs), ("kb", qs)], writes=[("ps", 6)])
            wq = (nbk - 1) * 128 + (128 if bis[-1] < nb - 1 else P)
            c0 = bis[0] * 128

            def fcq(e, c0=c0, wq=wq):
                e.activation(out=QT[0:64, 2 * g, c0:c0 + wq], in_=pview[0:64, 0:wq], func=AF.Copy)
                return e.activation(out=QT[64:128, 2 * g + 1, c0:c0 + wq], in_=pview[64:128, 0:wq], func=AF.Copy)
            S.add("act", fcq, reads=[("ps", 6)], writes=[("QT", g)])
            if not sample:
                def fck(e, c0=c0, wq=wq):
                    return e.tensor_copy(out=KT[:, g, kt_col0 + c0:kt_col0 + c0 + wq], in_=pview[:, 256:256 + wq])
                S.add("dve", fck, reads=[("ps", 6)], writes=[("KT", g, kt_col0 // 512)])

    score_rr = [0]

    def attention_prompt(ti, g):
        tiles = list(range(max(0, ti - 4), ti + 1))
        kblocks = [(tj, kbi) for tj in tiles for kbi in range(4)]
        NK = len(kblocks)
        seq_ = [(hh, n_i) for hh in range(2) for n_i in range(NK)]
        LA = 4
        pend = []
        sbanks = (0, 1, 2, 3, 7)

        def normalise(h, ab):
            rsl = h % 2

            def fn1(e, ab=ab, rsl=rsl):
                acc = ps[ab][:, 0:260].rearrange("p (q c) -> p q c", q=4)
                return e.reciprocal(out=rec[:, rsl, :], in_=acc[:, :, 64])
            S.add("dve", fn1, reads=[("ps", ab)], writes=[("rec", rsl)])

            def fn2(e, ab=ab, rsl=rsl, h=h):
                acc = ps[ab][:, 0:260].rearrange("p (q c) -> p q c", q=4)
                return e.tensor_tensor(out=attn[:, :, h * 64:(h + 1) * 64], in0=acc[:, :, 0:64],
                                       in1=rec[:, rsl, :].unsqueeze(2).to_broadcast([128, 4, 64]), op=ALU.mult)
            S.add("dve", fn2, reads=[("ps", ab), ("rec", rsl)], writes=[("attn", 0), ("attn", 1), ("attn", 2), ("attn", 3)])

        for idx in range(len(seq_) + LA):
            if idx < len(seq_):
                hh, n_i = seq_[idx]
                h = 2 * g + hh
                ab = 4 + h % 2
                tj, kbi = kblocks[n_i]
                dl = ti - tj
                if dl == 0:
                    qbs = [q for q in range(4) if q >= kbi]
                elif dl == 4:
                    qbs = [q for q in range(4) if q <= kbi]
                else:
                    qbs = [0, 1, 2, 3]
                q0, q1 = qbs[0], qbs[-1] + 1
                slot = tj % 5
                bk = sbanks[score_rr[0] % 5]
                score_rr[0] += 1
                kc0 = slot * 512 + kbi * 128

                def fs(e, bk=bk, kc0=kc0, q0=q0, q1=q1, h=h):
                    return e.matmul(ps[bk][:, q0 * 128:q1 * 128], lhsT=KT[:, g, kc0:kc0 + 128],
                                    rhs=QT[:, h, q0 * 128:q1 * 128], start=True, stop=True)
                S.add("pe", fs, reads=[("KT", g, slot), ("QT", g)], writes=[("ps", bk)])
                pslot = idx % 6

                def fe(e, bk=bk, pslot=pslot, q0=q0, q1=q1):
                    return e.activation(out=pT[:, pslot, q0 * 128:q1 * 128], in_=ps[bk][:, q0 * 128:q1 * 128],
                                        func=AF.Exp, scale=0.125)
                S.add("act", fe, reads=[("ps", bk)], writes=[("pT", pslot)])
                gs = 512 * dl - 128 * kbi + 384

                def fmk(e, pslot=pslot, q0=q0, q1=q1, gs=gs):
                    return e.tensor_tensor(out=pT[:, pslot, q0 * 128:q1 * 128], in0=pT[:, pslot, q0 * 128:q1 * 128],
                                           in1=G[:, gs + q0 * 128:gs + q1 * 128], op=ALU.mult)
                S.add("dve", fmk, reads=[("pT", pslot)], writes=[("pT", pslot)])
                vb = slot * 4 + kbi
                flags = []
                for q in qbs:
                    flags.append((q, (n_i == 0 and q == qbs[0]), (n_i == NK - 1 and q == qbs[-1])))
                pend.append((h, ab, pslot, vb, flags, n_i == NK - 1))
            if idx >= LA:
                h_, ab_, pslot_, vb_, flags_, last_ = pend[idx - LA]

                def fpv(e, pslot=pslot_, vb=vb_, flags=flags_, ab=ab_, h=h_):
                    ins = None
                    acc = ps[ab][:, 0:260].rearrange("p (q c) -> p q c", q=4)
                    for (q, st_, sp_) in flags:
                        ins = e.matmul(acc[:, q, :], lhsT=pT[:, pslot, q * 128:(q + 1) * 128],
                                       rhs=VX[:, vb, h, 0:65], start=st_, stop=sp_)
                    return ins
                S.add("pe", fpv, reads=[("pT", pslot_), ("VX", vb_, h_ // 4)], writes=[("ps", ab_)])
                if last_:
                    normalise(h_, ab_)

    def attn_norm_T(nb, P):
        for bi in range(nb):
            pw = 128 if bi < nb - 1 else P
            ys_ = bi % 2
            rms_rows(attn[0:pw, bi, :], pw, 512, ya[0:pw, ys_, :], [("attn", bi)], [("ya", ys_)])
            pview = ps[6][:, :].bitcast(BF16).rearrange("p (c t) -> p c t", c=8)

            def ft(e, ys_=ys_, pw=pw):
                for c in range(4):
                    ins = e.transpose(out=pview[:, c, 0:pw], in_=ya[0:pw, ys_, c * 128:(c + 1) * 128], identity=ident[:pw, :pw])
                return ins
            S.add("pe", ft, reads=[("ya", ys_)], writes=[("ps", 6)])

            def fc(e, bi=bi, pw=pw):
                gsl = gcol[:, 24:28].unsqueeze(2).to_broadcast([128, 4, pw])
                return e.tensor_tensor(out=yaT[:, :, bi * 128:bi * 128 + pw], in0=pview[:, 0:4, 0:pw], in1=gsl, op=ALU.mult)
            S.add("dve", fc, reads=[("ps", 6)], writes=[("yaT", bi)])

    def out_proj(slots, P, rc_col, rc_res, gbase_next):
        nb = len(slots)
        sls = [wload(wo[dq], "cvMo") for dq in range(4)]
        for bi, slot in enumerate(slots):
            pw = 128 if bi < nb - 1 else P
            for dq in range(4):
                sl = sls[dq]
                bk = big_bank()

                def fm(e, sl=sl, bk=bk, bi=bi, pw=pw):
                    for c in range(4):
                        e.matmul(ps[bk][0:pw, 0:256], lhsT=ycgT[:, c, bi * 128:bi * 128 + pw],
                                 rhs=wring[:, sl, c * 256:(c + 1) * 256], start=(c == 0), stop=(c == 3))
                    for c in range(4):
                        ins = e.matmul(ps[bk][0:pw, 256:512], lhsT=yaT[:, c, bi * 128:bi * 128 + pw],
                                       rhs=wring[:, sl, (4 + c) * 256:(5 + c) * 256], start=(c == 0), stop=(c == 3))
                    return ins
                S.add("pe", fm, reads=[("wr", sl), ("yaT", bi)] + [("ycgT", c) for c in range(4)], writes=[("ps", bk)])

                def fr(e, bk=bk, bi=bi, slot=slot, pw=pw, dq=dq):
                    xsl = xres[0:pw, slot, dq * 256:(dq + 1) * 256]
                    return e.scalar_tensor_tensor(out=xsl, in0=ps[bk][0:pw, 0:256], scalar=st[0:pw, rc_col + bi:rc_col + bi + 1],
                                                  in1=xsl, op0=ALU.mult, op1=ALU.add)
                S.add("dve", fr, reads=[("ps", bk), ("x", slot), rc_res], writes=[("x", slot)])

                def fr2(e, bk=bk, bi=bi, slot=slot, pw=pw, dq=dq):
                    xsl = xres[0:pw, slot, dq * 256:(dq + 1) * 256]
                    return e.tensor_tensor(out=xsl, in0=xsl, in1=ps[bk][0:pw, 256:512], op=ALU.add)
                S.add("dve", fr2, reads=[("ps", bk), ("x", slot)], writes=[("x", slot)])
            normA(bi, slot, pw)
            if bi >= 1:
                normB(bi - 1, 128, gbase_next)
        normB(nb - 1, 128 if nb > 1 else P, gbase_next)

    ost = [0]

    def final_norm_out(slots, P, dst_of, after_block=None):
        nb = len(slots)
        for bi, slot in enumerate(slots):
            pw = 128 if bi < nb - 1 else P
            c = stat_cols(3)
            m_ap, l_ap, r_ap = st[:pw, c:c + 1], st[:pw, c + 1:c + 2], st[:pw, c + 2:c + 3]
            sres = ("st", c)
            src = xres[0:pw, slot, :]

            def f1(e, src=src, m_ap=m_ap, pw=pw):
                return e.activation(out=junk[:pw, :], in_=src, func=AF.Square, scale=1.0 / 32, accum_out=m_ap)
            S.add("act", f1, reads=[("x", slot)], writes=[sres])

            def f2(e, m_ap=m_ap, l_ap=l_ap, pw=pw):
                return e.activation(out=l_ap, in_=m_ap, func=AF.Ln, bias=epsc[:pw, 0:1], scale=1.0)
            S.add("act", f2, reads=[sres], writes=[sres])

            def f3(e, l_ap=l_ap, r_ap=r_ap):
                return e.activation(out=r_ap, in_=l_ap, func=AF.Exp, scale=-0.5)
            S.add("act", f3, reads=[sres], writes=[sres])
            os_ = ost[0] % 2
            ost[0] += 1
            odst = attn[0:pw, 2 * os_:2 * os_ + 2, :].rearrange("p a b -> p (a b)")

            def f4(e, src=src, r_ap=r_ap, odst=odst, pw=pw):
                return e.scalar_tensor_tensor(out=odst, in0=src, scalar=r_ap, in1=gfb[0:pw, :], op0=ALU.mult, op1=ALU.mult)
            S.add("dve", f4, reads=[sres, ("x", slot)], writes=[("attn", 2 * os_), ("attn", 2 * os_ + 1)])
            dst = dst_of(bi)

            def fd(e, dst=dst, odst=odst, os_=os_):
                return e.dma_start(out=dst, in_=odst).then_inc(SEMS["yout%d" % os_], 16)
            S.add("pool", fd, reads=[("attn", 2 * os_), ("attn", 2 * os_ + 1)], dma_sem="yout%d" % os_)
            if after_block is not None:
                after_block(bi)

    cc_count = [0]

    def cache_copy_some(n):
        if not (cfg.do_sample and cfg.do_cachecopy):
            return
        total = NS * 2
        for _ in range(n):
            i = cc_count[0]
            if i >= total:
                return
            cc_count[0] += 1
            b, which = i // 2, i % 2
            src = (ck, cv)[which]
            dst = (cks, cvs)[which]
            def fn(e, b=b, src=src, dst=dst):
                e.dma_start(out=dst[b, 0:2032, :].rearrange("(a r) c -> a (r c)", a=16),
                            in_=src[b, 1:2033, :].rearrange("(a r) c -> a (r c)", a=16)).then_inc(SEMS["cc"], 16)
                return e.dma_start(out=dst[b, 2032:2047, :], in_=src[b, 2033:2048, :]).then_inc(SEMS["cc"], 16)
            S.add("sp", fn, dma_sem="cc", ndma=2)

    prenormed = [False]
    for n_t, (seq_i, ti) in enumerate(tiles):
        slots = pending_slots
        tok0 = ti * T
        cs_slot = n_t % 2

        def fcs(e, cs_slot=cs_slot, tok0=tok0):
            return e.dma_start(out=cs[:, cs_slot, :, :], in_=csp_d[tok0:tok0 + T, :].rearrange("(b p) c -> p b c", p=128)
                               ).then_inc(SEMS["cs%d" % cs_slot], 16)
        S.add("sp", fcs, writes=[("cs", cs_slot)], dma_sem="cs%d" % cs_slot)

        STG = cfg.stage
        if STG >= 1 and not prenormed[0]:
            norm_to_hT(slots, 128, 0)
        prenormed[0] = False
        if STG >= 2:
            ffn(0, slots, 128)
        next_slots = []
        if n_t + 1 < len(tiles):
            xload(tiles[n_t + 1][0], tiles[n_t + 1][1], (0, 1), next_slots)
        want_out = (tok0 >= SEQ - KEEP)
        ring = ti % 5
        if STG >= 3:
            norm_to_hT(slots, 128, 8)
            conv_path(slots, 128, ti == 0, False)
            rc_col, rc_res = conv_rstd(4, 128)
        if STG >= 4:
            vproj(slots, 128, lambda bi: ring * 4 + bi, False, seq_i, tok0, want_out)
        if STG >= 5:
            for g in range(4):
                qkproj(slots, 128, g, cs_slot, ring * 512, False)
                if g == 0:
                    cache_copy_some(2)
                if g >= 1 and STG >= 6:
                    attention_prompt(ti, g - 1)
            if STG >= 6:
                attention_prompt(ti, 3)
            if want_out:
                r0 = tok0 - (SEQ - KEEP)

                def fko(e, r0=r0, seq_i=seq_i):
                    return e.dma_start(out=ckp[seq_i, r0:r0 + T, :].rearrange("(b p) c -> p b c", p=128), in_=kout[:, :, :]
                                       ).then_inc(SEMS["kout"], 16)
                S.add("pool", fko, reads=[("kout", bi) for bi in range(4)], dma_sem="kout")
        if ti == NT - 1 and STG >= 3:
            def fsc(e, seq_i=seq_i):
                ins = None
                for j in range(4):
                    ins = e.dma_start(out=scp[seq_i, :, j * 128:(j + 1) * 128].rearrange("t p -> p t"),
                                      in_=ubuf[:, j, 512:514], allow_slow_non_contiguous=True).then_inc(SEMS["scp"], 16)
                return ins
            S.add("pool", fsc, reads=[("u", j) for j in range(4)], dma_sem="scp", ndma=4)
        if STG >= 7:
            attn_norm_T(4, 128)
            out_proj(slots, 128, rc_col, rc_res, 16)
        if STG >= 8:
            ffn(1, slots, 128)
        def after_blk(bdone, n_t=n_t, next_slots=next_slots):
            if n_t + 1 >= len(tiles):
                return
            if bdone in (0, 1):
                xload(tiles[n_t + 1][0], tiles[n_t + 1][1], (2 + bdone,), next_slots)
            if STG >= 8 and bdone == 1:
                for b_ in (0, 1):
                    normA(b_, next_slots[b_], 128)
                    normB(b_, 128, 0)
            if STG >= 8 and bdone == 3:
                for b_ in (2, 3):
                    normA(b_, next_slots[b_], 128)
                    normB(b_, 128, 0)
                prenormed[0] = True
        final_norm_out(slots, 128, lambda bi, seq_i=seq_i, tok0=tok0: yp[seq_i, tok0 + bi * 128:tok0 + (bi + 1) * 128, :],
                       after_block=after_blk)
        pending_slots = next_slots


    if cfg.do_sample:
        P = NS
        sslot = xslot_ctr[0] % 6
        xslot_ctr[0] += 1

        def fxs(e):
            return e.dma_start(out=xres[0:P, sslot, :], in_=xs[:, :]).then_inc(SEMS["x%d" % sslot], 16)
        S.add("pool", fxs, writes=[("x", sslot)], dma_sem="x%d" % sslot)
        slots = [sslot]
        norm_to_hT(slots, P, 0)
        ffn(0, slots, P)
        norm_to_hT(slots, P, 8)
        for j in range(4):
            def fst_(e, j=j):
                e.dma_start(out=ubuf[:, j, 128:128 + P], in_=sconv[:, 0, j * 128:(j + 1) * 128].rearrange("b p -> p b"),
                            allow_slow_non_contiguous=True).then_inc(SEMS["sst%d" % j], 16)
                return e.dma_start(out=ubuf[:, j, 256:256 + P], in_=sconv[:, 1, j * 128:(j + 1) * 128].rearrange("b p -> p b"),
                                   allow_slow_non_contiguous=True).then_inc(SEMS["sst%d" % j], 16)
            S.add("sp", fst_, reads=[("u", j)], writes=[("sst", j), ("u", j)], dma_sem="sst%d" % j, ndma=2)
        conv_path(slots, P, False, True)
        rc_col, rc_res = conv_rstd(1, P)
        def fsc0(e):
            return e.dma_start(out=scs[:, 0, :], in_=sconv[:, 1, :]).then_inc(SEMS["snew"], 16)
        S.add("pool", fsc0, dma_sem="snew")

        def fsc1(e):
            ins = None
            for j in range(4):
                ins = e.dma_start(out=scs[:, 1, j * 128:(j + 1) * 128].rearrange("b p -> p b"), in_=ubuf[:, j, 2:2 + P],
                                  allow_slow_non_contiguous=True).then_inc(SEMS["snew"], 16)
            return ins
        S.add("pool", fsc1, reads=[("u", j) for j in range(4)], dma_sem="snew", ndma=4)
        vproj(slots, P, None, True, 0, 0, False)
        for g in range(4):
            qkproj(slots, P, g, 0, 0, True)

        def fkn(e):
            return e.dma_start(out=cks[:, WBUF - 1, :], in_=kout[0:P, 0, :]).then_inc(SEMS["snew"], 16)
        S.add("pool", fkn, reads=[("kout", 0)], dma_sem="snew")

        gres = [("gT", fc) for fc in range(NFC)]
        gflat = gT[:, :, :].rearrange("p a b -> p (a b)")
        f32v = lambda a, n: gflat[:, a:a + 2 * n].bitcast(F32)
        Ks = [f32v(0, 512), f32v(1024, 512)]
        Vs = [f32v(2048, 512), f32v(3072, 512)]
        prods = [f32v(4096, 512), f32v(10240, 512)]
        Wbs = [gflat[:, 5120:5632], gflat[:, 6656:7168]]
        qbf = gflat[0:P, 5632:6144]
        scrs = [f32v(6144, 64), f32v(6272, 64)]
        pbfs = [gflat[:, 6400:6408], gflat[:, 6408:6416]]
        nsm = f32v(0, 600)

        def fqb(e):
            return e.tensor_copy(out=qbf, in_=qtok[0:P, :])
        S.add("dve", fqb, reads=[("qtok", g) for g in range(4)] + gres, writes=gres + ["qbf"])
        pats = [(128, 1), (512, 4), (2048, 16)]
        n_it = 0
        for b in range(P):
            bkq = big_bank()

            def fqbc(e, b=b, bkq=bkq):
                return e.matmul(ps[bkq][:, :], lhsT=ident[0:P, b:b + 1].to_broadcast([P, 128]), rhs=qbf,
                                start=True, stop=True)
            S.add("pe", fqbc, reads=["qbf"], writes=[("ps", bkq)])
            for pi, (w_, d_) in enumerate(pats):
                sl_ = n_it % 2
                start = WBUF - 128 * d_

                def fld(e, b=b, sl_=sl_, start=start, d_=d_):
                    e.dma_start(out=Ks[sl_], in_=ck[b, start:WBUF:d_, :]).then_inc(SEMS["ks%d" % sl_], 16)
                    return e.dma_start(out=Vs[sl_], in_=cv[b, start:WBUF:d_, :]).then_inc(SEMS["ks%d" % sl_], 16)
                S.add("sp", fld, writes=[("ksv", sl_)], dma_sem="ks%d" % sl_, ndma=2)

                prod, scr, Wb, pbf = prods[sl_], scrs[sl_], Wbs[sl_], pbfs[sl_]

                def fpr(e, sl_=sl_, bkq=bkq, prod=prod):
                    return e.tensor_tensor(out=prod, in0=Ks[sl_], in1=ps[bkq][:, :], op=ALU.mult)
                S.add("dve", fpr, reads=[("ksv", sl_), ("ps", bkq)], writes=[("prod", sl_)])

                def frd(e, prod=prod, scr=scr):
                    return e.tensor_reduce(out=scr[:, 0:8], in_=prod.rearrange("p (h d) -> p h d", h=8), axis=AX.X, op=ALU.add)
                S.add("dve", frd, reads=[("prod", sl_)], writes=[("scr0", sl_)])

                def fex(e, scr=scr):
                    return e.activation(out=scr[:, 8:16], in_=scr[:, 0:8], func=AF.Exp, scale=0.125)
                S.add("act", fex, reads=[("scr0", sl_)], writes=[("scr1", sl_)])

                def fw(e, sl_=sl_, scr=scr, Wb=Wb):
                    return e.tensor_tensor(out=Wb.rearrange("p (h d) -> p h d", h=8), in0=Vs[sl_].rearrange("p (h d) -> p h d", h=8),
                                           in1=scr[:, 8:16].unsqueeze(2).to_broadcast([128, 8, 64]), op=ALU.mult)
                S.add("dve", fw, reads=[("ksv", sl_), ("scr1", sl_)], writes=[("Wb", sl_)])

                def fpb(e, scr=scr, pbf=pbf):
                    return e.tensor_copy(out=pbf, in_=scr[:, 8:16])
                S.add("dve", fpb, reads=[("scr1", sl_)], writes=[("pbf", sl_)])
                first = (n_it == 0)
                last = (n_it == P * 3 - 1)

                def fac(e, b=b, first=first, last=last, Wb=Wb, pbf=pbf):
                    e.matmul(ps[4][0:P, :], lhsT=zsel[:, 15 - b:15 - b + P], rhs=Wb, start=first, stop=last)
                    return e.matmul(ps[5][0:P, 0:8], lhsT=zsel[:, 15 - b:15 - b + P], rhs=pbf, start=first, stop=last)
                S.add("pe", fac, reads=[("Wb", sl_), ("pbf", sl_)], writes=[("ps", 4), ("ps", 5)])
                n_it += 1
        nprod = nsm[0:P, 0:512]
        ns_s = nsm[0:P, 512:520]
        ns_p = nsm[0:P, 520:528]
        ns_d = nsm[0:P, 528:536]
        ns_r = nsm[0:P, 536:544]

        def fn_a(e):
            return e.tensor_tensor(out=nprod, in0=qtok[0:P, :], in1=kout[0:P, 0, :], op=ALU.mult)
        S.add("dve", fn_a, reads=[("qtok", g) for g in range(4)] + [("kout", 0)] + gres, writes=["nprod", ("ksv", 0), ("ksv", 1)])

        def fn_b(e):
            return e.tensor_reduce(out=ns_s, in_=nprod.rearrange("p (h d) -> p h d", h=8), axis=AX.X, op=ALU.add)
        S.add("dve", fn_b, reads=["nprod"], writes=["ns_s"])

        def fn_c(e):
            return e.activation(out=ns_p, in_=ns_s, func=AF.Exp, scale=0.125)
        S.add("act", fn_c, reads=["ns_s"], writes=["ns_p"])

        def fn_d(e):
            return e.tensor_tensor(out=nprod.rearrange("p (h d) -> p h d", h=8), in0=vnew[0:P, :].rearrange("p (h d) -> p h d", h=8),
                                   in1=ns_p.unsqueeze(2).to_broadcast([P, 8, 64]), op=ALU.mult)
        S.add("dve", fn_d, reads=["ns_p", ("vnew", 0), ("vnew", 1), "nprod"] + gres, writes=["nprod"])

        def fn_e(e):
            return e.scalar_tensor_tensor(out=nprod, in0=nprod, scalar=3.0, in1=ps[4][0:P, :], op0=ALU.mult, op1=ALU.add)
        S.add("dve", fn_e, reads=["nprod", ("ps", 4)], writes=["nprod"])

        def fn_f(e):
            return e.scalar_tensor_tensor(out=ns_d, in0=ns_p, scalar=3.0, in1=ps[5][0:P, 0:8], op0=ALU.mult, op1=ALU.add)
        S.add("dve", fn_f, reads=["ns_p", ("ps", 5)], writes=["ns_d"])

        def fn_g(e):
            return e.reciprocal(out=ns_r, in_=ns_d)
        S.add("dve", fn_g, reads=["ns_d"], writes=["ns_r"])

        def fn_h(e):
            return e.tensor_tensor(out=attn[0:P, 0, :].rearrange("p (h d) -> p h d", h=8), in0=nprod.rearrange("p (h d) -> p h d", h=8),
                                   in1=ns_r.unsqueeze(2).to_broadcast([P, 8, 64]), op=ALU.mult)
        S.add("dve", fn_h, reads=["ns_r", "nprod"], writes=[("attn", 0), ("attn", 1), ("attn", 2), ("attn", 3)] + gres)
        attn_norm_T(1, P)
        out_proj(slots, P, rc_col, rc_res, 16)
        ffn(1, slots, P)
        final_norm_out(slots, P, lambda bi: ys[:, :])

    cache_copy_some(1000)

    S.finalize()
    for k in S.sem_keys() + ["init"]:
        SEMS[k] = es.enter_context(nc.semaphore(k))
    finals = [(k, v) for k, v in S.dma_cnt.items()]
    const_total = S.dma_cnt.get("const", 0)
    with nc.Block() as block:
        @block.sync
        def _(e):
            S.emit("sp", e, SEMS)

        @block.gpsimd
        def _(e):
            S.emit("pool", e, SEMS, extra_final_waits=finals)

        @block.scalar
        def _(e):
            e.wait_ge(SEMS["const"], const_total)
            e.wait_ge(SEMS["init"], 1)
            S.emit("act", e, SEMS)

        @block.vector
        def _(e):
            init_fn(e).then_inc(SEMS["init"], 1)
            e.wait_ge(SEMS["const"], const_total)
            e.wait_ge(SEMS["init"], 1)
            S.emit("dve", e, SEMS)

        @block.tensor
        def _(e):
            e.wait_ge(SEMS["const"], const_total)
            e.wait_ge(SEMS["init"], 1)
            S.emit("pe", e, SEMS)
    es.close()
    return nc


def _consts(seq):
    ident = np.eye(128, dtype=np.float32)
    k = np.arange(128)[:, None]
    j = np.arange(GW)[None, :]
    dist = j - 384 - k
    f = np.zeros((128, GW), np.float32)
    for (w, d) in PATTERNS:
        f += ((dist >= 0) & (dist % d == 0) & (dist // d <= w // d)).astype(np.float32)
    half = 32
    inv = (10000.0 ** (-np.arange(half, dtype=np.float32) * 2.0 / 64)).astype(np.float32)
    pos = np.arange(seq, dtype=np.float32)
    ang = (pos[:, None] * inv[None, :]).astype(np.float32)
    csp = np.concatenate([np.cos(ang), np.sin(ang)], axis=1).astype(np.float32)
    angs = (np.float32(PAST_LEN) * inv)[None, :].astype(np.float32)
    css = np.concatenate([np.cos(angs), np.sin(angs)], axis=1).astype(np.float32)
    return dict(ident=ident.astype(ml_dtypes.bfloat16), identf=ident, gmask=f.astype(ml_dtypes.bfloat16),
                csp=csp, css=css)


def make_in_maps(cfg, ncores, inp):
    c = _consts(cfg.seq)
    g = lambda a: np.asarray(a, np.float32)
    gvec = np.zeros((128, 32), np.float32)
    gvec[:, 0:8] = g(inp["norm_ffn1"])[0].reshape(8, 128).T
    gvec[:, 8:16] = g(inp["norm_mix"])[0].reshape(8, 128).T
    gvec[:, 16:24] = g(inp["norm_ffn2"])[0].reshape(8, 128).T
    gvec[:, 24:28] = g(inp["out_norm_attn"])[0].reshape(4, 128).T
    gvec[:, 28:32] = g(inp["out_norm_conv"])[0].reshape(4, 128).T
    cw = g(inp["conv_w"])[0]
    cwv = np.ascontiguousarray(cw.reshape(3, 4, 128).transpose(2, 1, 0).reshape(128, 12))
    shared = dict(
        w1a=g(inp["ffn1_w1"])[0], w3a=g(inp["ffn1_w3"])[0], w2a=g(inp["ffn1_w2"])[0],
        w1b=g(inp["ffn2_w1"])[0], w3b=g(inp["ffn2_w3"])[0], w2b=g(inp["ffn2_w2"])[0],
        win=g(inp["w_in"])[0], wout=g(inp["w_out"])[0], gvec=gvec, cwv=cwv,
        gfin=g(inp["norm_final"]).reshape(1, D), **c)
    xp = g(inp["x_prompt"])
    xs = g(inp["x_sample"]).reshape(-1, D)
    sc = g(inp["state_conv"])[0]
    ckf = g(inp["cache_k"])[0].reshape(-1, WBUF, 512)
    cvf = g(inp["cache_v"])[0].reshape(-1, WBUF, 512)
    maps = []
    for i in range(ncores):
        m = dict(shared)
        m["xp"] = np.ascontiguousarray(xp[i * cfg.nseq:(i + 1) * cfg.nseq])
        m["xs"] = np.ascontiguousarray(xs[i * cfg.ns:(i + 1) * cfg.ns])
        m["sconv"] = np.ascontiguousarray(sc[i * cfg.ns:(i + 1) * cfg.ns])
        m["ck"] = np.ascontiguousarray(ckf[i * cfg.ns:(i + 1) * cfg.ns])
        m["cv"] = np.ascontiguousarray(cvf[i * cfg.ns:(i + 1) * cfg.ns])
        maps.append(m)
    return maps


def gather(cfg, ncores, results):
    cat = lambda k: np.concatenate([np.asarray(r[k]) for r in results], axis=0)
    yp = cat("yp")
    ys = cat("ys").reshape(ncores * cfg.ns, 1, D)
    scp = cat("scp")[None]
    scs = cat("scs")[None]
    ckp = cat("ckp").reshape(ncores * cfg.nseq, cfg.keep, 8, 64)[None]
    cvp = cat("cvp").reshape(ncores * cfg.nseq, cfg.keep, 8, 64)[None]
    cks = cat("cks").reshape(ncores * cfg.ns, WBUF, 8, 64)[None]
    cvs = cat("cvs").reshape(ncores * cfg.ns, WBUF, 8, 64)[None]
    return (yp, ys, scp, scs, ckp, cvp, cks, cvs)


def kernel(**inputs):
    cfg = Cfg()
    ncores = 8
    nc = build_program(cfg)
    maps = make_in_maps(cfg, ncores, inputs)
    res = run_bass_kernel_spmd(nc, maps, core_ids=list(range(ncores)))
    return gather(cfg, ncores, res.results)
```
